# Optimizing a Trainium2 kernel written in Bass

```python
import jax, jax.numpy as jnp
from jax import lax
import numpy as np

D_MODEL = 2048
BATCH = 4
SEQ = 4096
DEPTH = 2

N_MIXERS = 2
BLOCK = 128
NEG_INF = -1e30
RMS_EPS = 1e-6
FOX_HEAD_DIM = 64
FOX_HEADS = D_MODEL // FOX_HEAD_DIM
FOX_WIDTH = FOX_HEADS * FOX_HEAD_DIM
FOX_IN = 4 * FOX_WIDTH + FOX_HEADS
SWA_HEAD_DIM = 64
SWA_Q_HEADS = D_MODEL // SWA_HEAD_DIM
SWA_KV_HEADS = SWA_Q_HEADS // 8
SWA_GROUP = SWA_Q_HEADS // SWA_KV_HEADS
SWA_WINDOW = 128
SWA_WIDTH = SWA_Q_HEADS * SWA_HEAD_DIM
SWA_KV_WIDTH = SWA_KV_HEADS * SWA_HEAD_DIM
SWA_IN = 2 * SWA_WIDTH + 2 * SWA_KV_WIDTH
ROPE_THETA = 500000.0
ROT_DIM = SWA_HEAD_DIM // 4
N_FOX_LAYERS = (DEPTH + 1) // 2
N_SWA_LAYERS = DEPTH // 2

kernel_name = "fox_swa_sink_interleaved_gated_hybrid"


def rmsnorm(x, g):
    x32 = x.astype(jnp.float32)
    y = x32 * lax.rsqrt(jnp.mean(x32 * x32, axis=-1, keepdims=True) + RMS_EPS)
    return (y * g.astype(jnp.float32)).astype(x.dtype)


def partial_rope(x, pos):
    half = ROT_DIM // 2
    inv_freq = ROPE_THETA ** (-jnp.arange(half, dtype=jnp.float32) / half)
    ang = pos[:, None] * inv_freq[None, :]
    cos = jnp.cos(ang)[None, :, None, :]
    sin = jnp.sin(ang)[None, :, None, :]
    x32 = x.astype(jnp.float32)
    x1, x2 = x32[..., :half], x32[..., half:ROT_DIM]
    rot = jnp.concatenate([x1 * cos - x2 * sin, x2 * cos + x1 * sin], axis=-1)
    return jnp.concatenate([rot.astype(x.dtype), x[..., ROT_DIM:]], axis=-1)


def fox_attention(q, k, v, log_f):
    B, S, H, d = q.shape
    nb = S // BLOCK
    scale = d ** -0.5
    c = jnp.cumsum(log_f, axis=1).transpose(0, 2, 1)
    kf = k.astype(jnp.float32)
    key_pos = jnp.arange(S)
    qb = q.reshape(B, nb, BLOCK, H, d).transpose(1, 0, 2, 3, 4)
    cb = c.reshape(B, H, nb, BLOCK).transpose(2, 0, 1, 3)

    def one_block(args):
        qi, ci, i = args
        s = jnp.einsum('bqhd,bkhd->bhqk', qi.astype(jnp.float32), kf) * scale
        s = s + ci[..., None] - c[:, :, None, :]
        qpos = i * BLOCK + jnp.arange(BLOCK)
        mask = key_pos[None, :] <= qpos[:, None]
        s = jnp.where(mask[None, None], s, NEG_INF)
        p = jax.nn.softmax(s, axis=-1).astype(v.dtype)
        return jnp.einsum('bhqk,bkhd->bqhd', p, v)

    out = lax.map(one_block, (qb, cb, jnp.arange(nb)))
    return out.transpose(1, 0, 2, 3, 4).reshape(B, S, H * d)


def swa_attention(q, k, v, sinks):
    B, S = q.shape[:2]
    nb = S // BLOCK
    scale = SWA_HEAD_DIM ** -0.5
    qb = q.reshape(B, nb, BLOCK, SWA_KV_HEADS, SWA_GROUP, SWA_HEAD_DIM)

    def band(t):
        tp = jnp.pad(t, ((0, 0), (BLOCK, 0), (0, 0), (0, 0)))
        tp = tp.reshape(B, nb + 1, BLOCK, SWA_KV_HEADS, SWA_HEAD_DIM)
        return jnp.concatenate([tp[:, :-1], tp[:, 1:]], axis=2)

    kw, vw = band(k), band(v)
    s = jnp.einsum('bnqhgd,bnkhd->bnhgqk', qb.astype(jnp.float32), kw.astype(jnp.float32)) * scale
    t_loc = jnp.arange(BLOCK)[:, None]
    j_loc = jnp.arange(2 * BLOCK)[None, :]
    diff = t_loc + BLOCK - j_loc
    key_abs = (jnp.arange(nb)[:, None, None] - 1) * BLOCK + j_loc[None]
    mask = (diff >= 0)[None] & (diff < SWA_WINDOW)[None] & (key_abs >= 0)
    s = jnp.where(mask[None, :, None, None], s, NEG_INF)
    sink = sinks.astype(jnp.float32).reshape(SWA_KV_HEADS, SWA_GROUP)[None, None, :, :, None, None]
    m = jnp.maximum(jnp.max(s, axis=-1, keepdims=True), sink)
    e = jnp.exp(s - m)
    denom = jnp.sum(e, axis=-1, keepdims=True) + jnp.exp(sink - m)
    p = (e / denom).astype(v.dtype)
    o = jnp.einsum('bnhgqk,bnkhd->bnqhgd', p, vw)
    return o.reshape(B, S, SWA_WIDTH)


def setup_inputs(seed: int = 0) -> dict:
    key = jax.random.key(seed)
    ks = jax.random.split(key, 10)
    x = jax.random.normal(ks[0], (BATCH, SEQ, D_MODEL), jnp.float32)
    norm_g = 1.0 + 0.02 * jax.random.normal(ks[1], (DEPTH, D_MODEL), jnp.float32)
    fox_w_in = jax.random.normal(ks[2], (N_FOX_LAYERS, D_MODEL, FOX_IN), jnp.float32) * D_MODEL ** -0.5
    fox_b_f = (jnp.linspace(1.0, 6.0, FOX_HEADS, dtype=jnp.float32)[None, :]
               + 0.1 * jax.random.normal(ks[3], (N_FOX_LAYERS, FOX_HEADS), jnp.float32))
    fox_w_out = jax.random.normal(ks[4], (N_FOX_LAYERS, FOX_WIDTH, D_MODEL), jnp.float32) * FOX_WIDTH ** -0.5
    swa_w_in = jax.random.normal(ks[5], (N_SWA_LAYERS, D_MODEL, SWA_IN), jnp.float32) * D_MODEL ** -0.5
    swa_sinks = 0.5 * jax.random.normal(ks[6], (N_SWA_LAYERS, SWA_Q_HEADS), jnp.float32)
    swa_w_out = jax.random.normal(ks[7], (N_SWA_LAYERS, SWA_WIDTH, D_MODEL), jnp.float32) * SWA_WIDTH ** -0.5
    final_g = 1.0 + 0.02 * jax.random.normal(ks[8], (D_MODEL,), jnp.float32)
    return {"x": x, "norm_g": norm_g, "fox_w_in": fox_w_in, "fox_b_f": fox_b_f,
            "fox_w_out": fox_w_out, "swa_w_in": swa_w_in, "swa_sinks": swa_sinks,
            "swa_w_out": swa_w_out, "final_g": final_g}


def reference(x, norm_g, fox_w_in, fox_b_f, fox_w_out, swa_w_in, swa_sinks, swa_w_out, final_g):
    B, S, _ = x.shape
    pos = jnp.arange(S, dtype=jnp.float32)
    for i in range(DEPTH):
        h = rmsnorm(x, norm_g[i])
        j = i // N_MIXERS
        if i % N_MIXERS == 0:
            p = h @ fox_w_in[j]
            W = FOX_WIDTH
            q = p[..., :W].reshape(B, S, FOX_HEADS, FOX_HEAD_DIM)
            k = p[..., W:2 * W].reshape(B, S, FOX_HEADS, FOX_HEAD_DIM)
            v = p[..., 2 * W:3 * W].reshape(B, S, FOX_HEADS, FOX_HEAD_DIM)
            gate = p[..., 3 * W:4 * W]
            log_f = jax.nn.log_sigmoid(p[..., 4 * W:].astype(jnp.float32) + fox_b_f[j].astype(jnp.float32))
            y = fox_attention(q, k, v, log_f)
            w_out = fox_w_out[j]
        else:
            p = h @ swa_w_in[j]
            WQ, WK = SWA_WIDTH, SWA_KV_WIDTH
            q = p[..., :WQ].reshape(B, S, SWA_Q_HEADS, SWA_HEAD_DIM)
            k = p[..., WQ:WQ + WK].reshape(B, S, SWA_KV_HEADS, SWA_HEAD_DIM)
            v = p[..., WQ + WK:WQ + 2 * WK].reshape(B, S, SWA_KV_HEADS, SWA_HEAD_DIM)
            gate = p[..., WQ + 2 * WK:]
            q = partial_rope(q, pos)
            k = partial_rope(k, pos)
            y = swa_attention(q, k, v, swa_sinks[j])
            w_out = swa_w_out[j]
        y = y * jax.nn.silu(gate)
        x = x + y @ w_out
    return rmsnorm(x, final_g)
```

```python
import numpy as np
import ml_dtypes
from contextlib import ExitStack
import concourse.bass as bass
import concourse.mybir as mybir
from concourse.bass_utils import run_bass_kernel_spmd

F32 = mybir.dt.float32
BF16 = mybir.dt.bfloat16
AF = mybir.ActivationFunctionType
ALU = mybir.AluOpType
NPBF = ml_dtypes.bfloat16

F_BATT = True
F_BNORM = True
D = 2048
KC = 16
EPS = 1e-6
NEG = -30000.0


class Sem:
    def __init__(self, h, name):
        self.h = h
        self.name = name
        self.n = 0


class _Cut(Exception):
    pass


class KB:
    ENG = ("pe", "act", "dve", "pool", "sp")
    dbg = None

    def cut(self, level, waits):
        if self.dbg == level:
            self._waits("sp", waits)
            self.flush()
            raise _Cut()

    def __init__(self, nc):
        self.nc = nc
        self.es = ExitStack()
        self.q = {e: [] for e in self.ENG}
        self.waited = {e: {} for e in self.ENG}
        self.esem = {}
        for e in ("pe", "act", "dve", "pool"):
            self.esem[e] = self.sem("e_" + e)

    prefix = ""
    cur = None

    def sem(self, name):
        name = self.prefix + name
        return Sem(self.es.enter_context(self.nc.semaphore(name)), name)

    def sb(self, name, shape, dt):
        return (self.cur or self.es).enter_context(self.nc.sbuf_tensor(self.prefix + name, shape, dt))

    def ps(self, name, shape, dt):
        return (self.cur or self.es).enter_context(self.nc.psum_tensor(self.prefix + name, shape, dt))

    def raw(self, eng, fn, sem, n, waits=()):
        self._waits(eng, waits)
        sem.n += n
        self.q[eng].append((1, fn, sem.h, n))
        return (sem, sem.n)

    def barrier(self, toks):
        for eng in self.ENG:
            self._waits(eng, toks)

    def _waits(self, eng, waits):
        for w in waits:
            if w is None:
                continue
            s, v = w
            if v <= 0 or self.waited[eng].get(s.name, 0) >= v:
                continue
            self.waited[eng][s.name] = v
            self.q[eng].append((0, s.h, v))

    def op(self, eng, fn, waits=(), tok=False):
        self._waits(eng, waits)
        if tok:
            s = self.esem[eng]
            s.n += 1
            self.q[eng].append((1, fn, s.h, 1))
            return (s, s.n)
        self.q[eng].append((1, fn, None, 0))
        return None

    def dma(self, eng, out, in_, sem, waits=()):
        self._waits(eng, waits)
        sem.n += 16
        self.q[eng].append((1, (lambda e: e.dma_start(out=out, in_=in_)), sem.h, 16))
        return (sem, sem.n)

    def flush(self):
        with self.nc.Block() as block:
            dec = {"pe": block.tensor, "act": block.scalar, "dve": block.vector,
                   "pool": block.gpsimd, "sp": block.sync}
            for eng in self.ENG:
                items = self.q[eng]
                self.q[eng] = []
                if not items:
                    continue

                def body(e, items=items):
                    for it in items:
                        if it[0] == 0:
                            e.wait_ge(it[1], it[2])
                        else:
                            ins = it[1](e)
                            if it[2] is not None:
                                ins.then_inc(it[2], it[3])

                dec[eng](body)


def MM(out, lhsT, rhs, start, stop):
    return lambda e: e.matmul(out, lhsT=lhsT, rhs=rhs, start=start, stop=stop, skip_group_check=True)


def TR(out, in_, ident):
    return lambda e: e.transpose(out, in_, ident)


def ACT(out, in_, func, **kw):
    return lambda e: e.activation(out=out, in_=in_, func=func, **kw)


def TT(out, in0, in1, op):
    return lambda e: e.tensor_tensor(out=out, in0=in0, in1=in1, op=op)


def TS(out, in0, s1, op0, s2=None, op1=None):
    if op1 is None:
        return lambda e: e.tensor_scalar(out=out, in0=in0, scalar1=s1, scalar2=None, op0=op0)
    return lambda e: e.tensor_scalar(out=out, in0=in0, scalar1=s1, scalar2=s2, op0=op0, op1=op1)


def STT(out, in0, scalar, in1, op0, op1):
    return lambda e: e.scalar_tensor_tensor(out=out, in0=in0, scalar=scalar, in1=in1, op0=op0, op1=op1)


def CP(out, in_):
    return lambda e: e.tensor_copy(out=out, in_=in_)


def RCP(out, in_):
    return lambda e: e.reciprocal(out=out, in_=in_)


def MSET(ap, c):
    return lambda e: e.memset(ap, c)


class NormT:
    def __init__(self, K, gbc, ident, tp2, name, nhb=2):
        self.K = K
        self.gbc = gbc
        self.ident = ident
        self.tp2 = tp2
        self.junk = K.sb(name + "_junk", [128, D], BF16)
        self.hb = [K.sb(name + "_hb%d" % i, [128, D], BF16) for i in range(nhb)]
        self.ss = K.sb(name + "_ss", [128, 64], F32)
        self.std = K.sb(name + "_std", [128, 64], F32)
        self.rstd = K.sb(name + "_rstd", [128, 64], F32)
        self.n = 0
        self.sc = 0
        self.t_tr = {}
        self.t_cp = {}
        self.t_stt = {}

    def stats_act(self, xs, waits):
        K = self.K
        col = self.sc % 64
        self.sc += 1
        t = K.op("act", ACT(self.junk[:], xs, AF.Square, accum_out=self.ss[:, col:col + 1]),
                 waits=waits, tok=True)
        t = K.op("act", ACT(self.std[:, col:col + 1], self.ss[:, col:col + 1], AF.Sqrt,
                            scale=1.0 / D, bias=EPS), waits=[t], tok=True)
        return t, col

    def stats_dve(self, tc):
        t, col = tc
        t = self.K.op("dve", RCP(self.rstd[:, col:col + 1], self.std[:, col:col + 1]), waits=[t], tok=True)
        return t, col

    def stats(self, xs, waits):
        return self.stats_dve(self.stats_act(xs, waits))

    def emit(self, xs, dst, waits, dst_waits=(), pre=None):
        K = self.K
        n = self.n
        t, col = pre if pre is not None else self.stats(xs, waits)
        K.cut(-4, [t])
        nh = len(self.hb)
        ntp = len(self.tp2)
        hb = self.hb[n % nh]
        tp = self.tp2[n % ntp]
        t_stt = K.op("dve", STT(hb[:], xs, self.rstd[:, col:col + 1], self.gbc[:],
                                ALU.mult, ALU.mult),
                     waits=[t, self.t_tr.get(n - nh)], tok=True)
        self.t_stt[n] = t_stt
        K.cut(-3, [t_stt])
        tok = None
        for kc in range(KC):
            tok = K.op("pe", TR(tp[:, kc, :], hb[:, kc * 128:(kc + 1) * 128], self.ident),
                       waits=[t_stt, self.t_cp.get(n - ntp)] if kc == 0 else (), tok=(kc == KC - 1))
        self.t_tr[n] = tok
        t_cp = K.op("act", ACT(dst, tp[:], AF.Copy), waits=[tok] + list(dst_waits), tok=True)
        self.t_cp[n] = t_cp
        K.cut(-2, [t_cp])
        self.n += 1
        return t_cp


def build_A(S=4096, NP=8, dbg=9):
    NB = S // 128
    NCH = S // 512
    NH = 2 * NP
    nc = bass.Bass("TRN2", target_bir_lowering=False)
    x = nc.dram_tensor("x", [S, D], F32, kind="ExternalInput").ap()
    g0 = nc.dram_tensor("g0", [1, D], F32, kind="ExternalInput").ap()
    wA = nc.dram_tensor("wA", [NP, 128, KC, 512], F32, kind="ExternalInput").ap()
    wf = nc.dram_tensor("wf", [128, KC, 16], F32, kind="ExternalInput").ap()
    bfv = nc.dram_tensor("bf", [16, 1], F32, kind="ExternalInput").ap()
    cst = nc.dram_tensor("cst", [128, 256], BF16, kind="ExternalInput").ap()
    y0T = nc.dram_tensor("y0T", [NP, 128, S], BF16, kind="ExternalOutput").ap()
    hT_d = nc.dram_tensor("hT_d", [NCH, 128, KC, 512], BF16).ap()
    caug = nc.dram_tensor("caug", [16, 2, 3, S], BF16).ap()

    K = KB(nc)
    K.dbg = dbg
    try:
      with K.es:
        _build_A_body(K, nc, S, NP, NB, NCH, x, g0, wA, wf, bfv, cst, y0T, hT_d, caug, dbg)
    except _Cut:
        pass
    return nc


def _build_A_body(K, nc, S, NP, NB, NCH, x, g0, wA, wf, bfv, cst, y0T, hT_d, caug, dbg, fz=None):
    if True:
        cst_sb = K.sb("cst_sb", [128, 256], BF16)
        ident = cst_sb[:, 0:128]
        cmask = cst_sb[:, 128:256]
        gbc = K.sb("gbc", [128, D], F32)
        NXB = 3
        xbuf = [K.sb("xbuf%d" % i, [128, D], F32) for i in range(NXB)]
        hbuf = [K.sb("hbuf%d" % i, [128, KC, 512], BF16) for i in range(2)]
        wfb = K.sb("wfb", [128, KC, 16], BF16)
        nb = K.sb("nb", [16, 2], F32)
        ef = K.sb("ef", [16, 512], F32)
        lsp = K.sb("lsp", [16, 512], F32)
        ones16 = K.sb("ones16", [16, 512], F32)
        Ec = [K.sb("Ec%d" % i, [16, 512], F32) for i in range(2)]
        e8 = K.sb("e8", [16, 512], F32)
        r1 = K.sb("r1", [16, 512], F32)
        r2 = K.sb("r2", [16, 512], F32)
        TKt = K.sb("TKt", [16, 3, 512], BF16)
        TQt = K.sb("TQt", [16, 3, 512], BF16)
        wbuf = [K.sb("wbuf%d" % i, [128, KC, 512], BF16) for i in range(2)]
        QA = K.sb("QA", [128, S], BF16)
        QB = K.sb("QB", [128, S], BF16)
        KA = K.sb("KA", [128, S], BF16)
        KBt = K.sb("KBt", [128, S], BF16)
        VA = K.sb("VA", [128, NB, 128], BF16)
        VB = K.sb("VB", [128, NB, 128], BF16)
        sg = K.sb("sg", [128, S], BF16)
        NSB = 3
        pt = [K.sb("pt%d" % i, [128, 2, 512], BF16) for i in range(NSB)]
        rt = K.sb("rt", [128, 512], F32)
        t1 = K.sb("t1", [128, 512], F32)
        yp = [K.sb("yp%d" % i, [128, S], BF16) for i in range(1)] * 2
        st = K.ps("st", [128, 2, 2, 512], F32)
        ob = K.ps("ob", [128, 2, 512], F32)
        ip = K.ps("ip", [128, 2, 512], F32)
        tp2 = [st[:, i].rearrange("p a b -> p (a b)").bitcast(BF16).rearrange("p (k t) -> p k t", k=KC)
               for i in range(2)]
        fps = ob[0:16, 0, :]
        stv = [st[:, 0], st[:, 1], ip[:]]
        s_c = K.sem("s_c")
        s_c2 = K.sem("s_c2")
        s_x = [K.sem("s_x%d" % i) for i in range(3)]
        s_hst = [K.sem("s_hst%d" % i) for i in range(2)]
        s_cst = K.sem("s_cst")
        s_w = [K.sem("s_w%d" % i) for i in range(2)]
        s_h = [K.sem("s_h%d" % i) for i in range(2)]
        s_aug = K.sem("s_aug")
        s_y = [K.sem("s_y%d" % i) for i in range(2)]

        t_c0 = K.dma("sp", cst_sb[:], cst, s_c)
        K.dma("sp", gbc[:], g0.partition_broadcast(128), s_c)
        t_c = K.dma("sp", nb[:, 0:1], bfv, s_c)
        t_c2 = K.dma("pool", wfb[:], wf, s_c2)
        t_nb = K.op("dve", TS(nb[:, 1:2], nb[:, 0:1], -1.0, ALU.mult), waits=[t_c], tok=True)
        K.op("dve", MSET(ones16[:], 1.0))
        K.op("pool", MSET(QA[64:70, :], 1.0))
        K.op("pool", MSET(KA[64:70, :], 1.0))
        K.op("pool", MSET(QB[0:64, :], 0.0))
        K.op("pool", MSET(KBt[0:64, :], 0.0))
        K.op("pool", MSET(VA[:, :, 64:128], 1.0))
        K.op("pool", MSET(VB[:, :, 0:64], 1.0), tok=True)
        K.op("pool", MSET(QB[0:6, :], 1.0), waits=[(K.esem["pool"], K.esem["pool"].n)])
        t_set = K.op("pool", MSET(KBt[0:6, :], 1.0), tok=True)
        if fz is not None:
            half = S // 2
            SECL = half + 128
            agin, agout = fz["agin"], fz["agout"]
            zt = K.sb("zt", [128, NP, 128], BF16)
            s_z = K.sem("s_z")
            tz = K.op("dve", MSET(zt[:], 0.0), tok=True)
        K.cut(-5, [t_set, t_nb, t_c0, t_c2])

        NT = NormT(K, gbc, ident[:, :] if False else ident, tp2, "n0")
        t_hst = {}
        t_fexp = {}
        t_cstore = {}
        t_scan = {}
        t_fmm = {}
        pre = {}
        t_xld = {}

        def x_load(b):
            if b < NB:
                t_xld[b] = K.dma("sp", xbuf[b % NXB][:], x[b * 128:(b + 1) * 128, :], s_x[b % NXB],
                                 waits=[NT.t_stt.get(b - NXB)])

        def x_stats_act(b):
            return NT.stats_act(xbuf[b % NXB][:], [t_xld[b], t_c]) if b < NB else None

        x_load(0)
        x_load(1)
        pre[0] = NT.stats_dve(x_stats_act(0))
        for b in range(NB):
            c, bi = b // 4, b % 4
            xs = xbuf[b % NXB]
            x_load(b + 2)
            sa = x_stats_act(b + 1)
            t_cp = NT.emit(xs[:], hbuf[c % 2][:, :, bi * 128:(bi + 1) * 128], waits=[],
                           dst_waits=[t_hst.get(c - 2), t_fmm.get(c - 2)] if bi == 0 else (), pre=pre[b])
            if sa is not None:
                pre[b + 1] = NT.stats_dve(sa)
            if bi != 3:
                continue
            t_hst[c] = K.dma("sp", hT_d[c], hbuf[c % 2][:], s_hst[c % 2], waits=[t_cp])
            tok = None
            for kc in range(KC):
                tok = K.op("pe", MM(fps, wfb[:, kc, :], hbuf[c % 2][:, kc, :], kc == 0, kc == KC - 1),
                           waits=[t_cp, t_c2, t_fexp.get(c - 1)] if kc == 0 else (), tok=(kc == KC - 1))
            t_fmm[c] = tok
            K.cut(-1, [tok, t_hst[c]])
            t_fexp[c] = K.op("act", ACT(ef[:], fps, AF.Exp, scale=-1.0, bias=nb[:, 1:2]),
                             waits=[tok, t_nb, t_scan.get(c - 1)], tok=True)
            K.cut(10, [t_fexp[c]])
            t = K.op("act", ACT(lsp[:], ef[:], AF.Ln, bias=1.0, scale=1.0), waits=[t_fexp[c]], tok=True)
            K.cut(11, [t])
            init = 0.0 if c == 0 else Ec[(c - 1) % 2][:, 511:512]
            t = K.op("dve", (lambda o, i: (lambda e: e.tensor_tensor_scan(
                out=o, data0=ones16[:], data1=lsp[:], initial=i, op0=ALU.mult, op1=ALU.add)))(Ec[c % 2][:], init),
                waits=[t], tok=True)
            t_scan[c] = t
            K.cut(12, [t])
            t = K.op("dve", TS(e8[:], Ec[c % 2][:], 8.0, ALU.mult), waits=[t, t_cstore.get(c - 1)], tok=True)
            t = K.op("dve", CP(TKt[:, 0, :], e8[:]), waits=[t], tok=True)
            t = K.op("dve", TT(r1[:], e8[:], TKt[:, 0, :], ALU.subtract), waits=[t], tok=True)
            t = K.op("dve", CP(TKt[:, 1, :], r1[:]), waits=[t], tok=True)
            t = K.op("dve", TT(r2[:], r1[:], TKt[:, 1, :], ALU.subtract), waits=[t], tok=True)
            t = K.op("dve", CP(TKt[:, 2, :], r2[:]), waits=[t], tok=True)
            t = K.op("dve", TS(TQt[:], TKt[:], -1.0, ALU.mult), waits=[t], tok=True)
            K.cut(13, [t])
            K.dma("sp", caug[:, 0, :, c * 512:(c + 1) * 512], TQt[:], s_cst, waits=[t])
            t_cstore[c] = K.dma("sp", caug[:, 1, :, c * 512:(c + 1) * 512], TKt[:], s_cst, waits=[t])
        t_A0 = [t_hst[NCH - 1], t_hst.get(NCH - 2), t_cstore[NCH - 1], t_fmm[NCH - 1]]
        if fz is not None:
            t_zero = K.dma("sp", agin[:, 0, :, 0:128].rearrange("p f t -> f p t"), zt[:], s_z, waits=[tz])

        if True:
            K.cut(0, t_A0)
        ipn = 0
        ev = {}
        hcn = 0
        t_pechunk = {}
        tcnt = 0
        gcnt = 0
        t_exp = {}
        t_pv = {}
        t_rel = {}
        t_y = None
        t_att_pe = None
        t_wload = {}
        t_ystore = {}

        def load_w(p):
            t = None
            for k4 in range(8):
                t = K.dma("pool", wbuf[p % 2][:, k4 * 2:(k4 + 1) * 2, :], wA[p, :, k4 * 2:(k4 + 1) * 2, :],
                          s_w[p % 2], waits=[t_wfree.get(p - 2)])
            t_wload[p] = t

        t_wfree = {}
        t_hload = {}

        def h_load(i):
            if i < NP * NCH:
                t_hload[i] = K.dma("sp", hbuf[i % 2][:], hT_d[i % NCH], s_h[i % 2], waits=[t_pechunk.get(i - 2)] + t_A0)

        load_w(0)
        for p in range(NP):
            hA, hB = 2 * p, 2 * p + 1
            if p + 1 < NP:
                load_w(p + 1)
            if fz is not None:
                fz["emit_P"](p)
            wa = [t_att_pe, t_set] + t_A0
            K.dma("sp", QA[64:67, :], caug[hA, 0], s_aug, waits=wa)
            K.dma("sp", KA[67:70, :], caug[hA, 1], s_aug)
            K.dma("sp", QB[0:3, :], caug[hB, 0], s_aug)
            t_aug = K.dma("sp", KBt[3:6, :], caug[hB, 1], s_aug)
            evw = [t_att_pe, t_y, t_set]
            K.cut(20, [t_aug, t_wload[p]])
            for c in range(NCH):
                hb_ = hbuf[hcn % 2]
                if hcn == 0:
                    h_load(0)
                    h_load(1)
                t_hl = t_hload[hcn]
                cols = slice(c * 512, (c + 1) * 512)
                for gi, wo in enumerate((0, 128, 384)):
                    acc = ip[:, ipn % 2, :]
                    tok = None
                    for kc in range(KC):
                        tok = K.op("pe", MM(acc, wbuf[p % 2][:, kc, wo:wo + 128], hb_[:, kc, :], kc == 0, kc == KC - 1),
                                   waits=[t_hl, t_wload[p]] + list(ev.get(ipn - 2, ())) if kc == 0 else (),
                                   tok=(kc == KC - 1))
                    if gi == 0:
                        a = K.op("act", ACT(QA[0:64, cols], acc[0:64, :], AF.Copy), waits=[tok] + evw, tok=True)
                        d = K.op("dve", CP(QB[64:128, cols], acc[64:128, :]), waits=[tok] + evw, tok=True)
                        ev[ipn] = (a, d)
                    elif gi == 1:
                        a = K.op("act", ACT(KA[0:64, cols], acc[0:64, :], AF.Copy), waits=[tok] + evw, tok=True)
                        d = K.op("dve", CP(KBt[64:128, cols], acc[64:128, :]), waits=[tok] + evw, tok=True)
                        ev[ipn] = (a, d)
                    else:
                        a = K.op("act", ACT(sg[:, cols], acc, AF.Silu), waits=[tok] + evw, tok=True)
                        ev[ipn] = (a,)
                    ipn += 1
                    K.cut(21 + gi, list(ev[ipn - 1]))
                acc = ip[:, ipn % 2, :].rearrange("p (b f) -> p b f", b=4)
                tok = None
                for bi in range(4):
                    for kc in range(KC):
                        last = (bi == 3 and kc == KC - 1)
                        tok = K.op("pe", MM(acc[:, bi, :], hb_[:, kc, bi * 128:(bi + 1) * 128],
                                            wbuf[p % 2][:, kc, 256:384], kc == 0, kc == KC - 1),
                                   waits=list(ev.get(ipn - 2, ())) if (bi == 0 and kc == 0) else (), tok=last)
                t_pechunk[hcn] = tok
                K.cut(24, [tok])
                a = K.op("act", ACT(VA[:, 4 * c:4 * c + 4, 0:64], acc[:, :, 0:64], AF.Copy), waits=[tok] + evw, tok=True)
                K.cut(25, [a])
                a = K.op("act", ACT(VB[:, 4 * c:4 * c + 4, 64:128], acc[:, :, 64:128], AF.Copy), tok=True)
                ev[ipn] = (a,)
                ipn += 1
                h_load(hcn + 2)
                hcn += 1
            t_wfree[p] = t_pechunk[hcn - 1]
            ev_done = list(ev[ipn - 1]) + list(ev[ipn - 2]) + list(ev[ipn - 3]) + list(ev[ipn - 4])

            K.cut(1, ev_done)
            tiles = []
            for X in (0, 1):
                for G in range(NCH):
                    for jj in range(2 * G + 2):
                        tiles.append((X, G, jj))

            def emit_qk(X, G, jj, n):
                Qt, Kt = (QA, KA) if X == 0 else (QB, KBt)
                rows = slice(0, 70) if X == 0 else slice(0, 128)
                sb_ = stv[n % NSB]
                tok = None
                first = True
                for sl in range(2):
                    j = 2 * jj + sl
                    r = j - 4 * G
                    w0 = [t_exp.get(n - NSB), t_aug] + ev_done if first else ()
                    first = False
                    kcols = slice(j * 128, (j + 1) * 128)
                    if r < 0:
                        tok = K.op("pe", MM(sb_[:, sl, :], Kt[rows, kcols], Qt[rows, G * 512:(G + 1) * 512], True, True),
                                   waits=w0, tok=(sl == 1))
                    else:
                        dsl = slice(r * 128, (r + 1) * 128)
                        K.op("pe", MM(sb_[:, sl, dsl], ident, cmask, True, False), waits=w0)
                        tok = K.op("pe", MM(sb_[:, sl, dsl], Kt[rows, kcols],
                                            Qt[rows, G * 512 + r * 128:G * 512 + (r + 1) * 128], False, True),
                                   tok=(sl == 1 and r == 3))
                        if r < 3:
                            tok = K.op("pe", MM(sb_[:, sl, (r + 1) * 128:512], Kt[rows, kcols],
                                                Qt[rows, G * 512 + (r + 1) * 128:(G + 1) * 512], True, True),
                                       tok=(sl == 1))
                return tok

            def emit_exp(X, G, jj, n, t_qk):
                buf = n % NSB
                sb_ = stv[buf]
                w = [t_qk, t_pv.get(n - NSB)]
                if 2 * jj + 1 < 4 * G:
                    return K.op("act", ACT(pt[buf][:], sb_, AF.Exp, scale=0.125), waits=w, tok=True)
                tok = None
                for sl in range(2):
                    r = 2 * jj + sl - 4 * G
                    tok = K.op("act", ACT(pt[buf][:, sl, r * 128:512], sb_[:, sl, r * 128:512], AF.Exp, scale=0.125),
                               waits=w if sl == 0 else (), tok=(sl == 1))
                return tok

            def emit_pv(X, G, jj, n, t_e, g):
                Vt = VA if X == 0 else VB
                buf = n % NSB
                tok = None
                for sl in range(2):
                    j = 2 * jj + sl
                    r = max(0, j - 4 * G)
                    w = [t_e, t_rel.get(g - 2)] if sl == 0 else ()
                    tok = K.op("pe", MM(ob[:, g % 2, r * 128:512], Vt[:, j, :], pt[buf][:, sl, r * 128:512],
                                        j == 0, j == 4 * G + 3), waits=w, tok=(sl == 1))
                return tok

            def emit_norm(X, G, g, t_last):
                nonlocal t_y
                o = ob[:, g % 2, :]
                cols = slice(G * 512, (G + 1) * 512)
                ro, rd = (slice(0, 64), slice(64, 128)) if X == 0 else (slice(64, 128), slice(0, 64))
                t = K.op("dve", RCP(rt[rd, :], o[rd, :]), waits=[t_last, t_y], tok=True)
                t = K.op("dve", TT(t1[ro, :], o[ro, :], rt[rd, :], ALU.mult), waits=[t], tok=True)
                t_rel[g] = t
                t_y = K.op("dve", TT(yp[p % 2][ro, cols], t1[ro, :], sg[ro, cols], ALU.mult),
                           waits=[t, t_ystore.get(p - 1)], tok=True)

            NTI = len(tiles)
            qk_tok = {}
            for i0 in range(min(NSB - 1, NTI)):
                qk_tok[i0] = emit_qk(*tiles[i0], tcnt + i0)
            for i in range(NTI):
                X, G, jj = tiles[i]
                n = tcnt + i
                if i + NSB - 1 < NTI:
                    qk_tok[i + NSB - 1] = emit_qk(*tiles[i + NSB - 1], n + NSB - 1)
                t_exp[n] = emit_exp(X, G, jj, n, qk_tok[i])
                g = gcnt + X * NCH + G
                t_pv[n] = emit_pv(X, G, jj, n, t_exp[n], g)
                if jj == 2 * G + 1:
                    emit_norm(X, G, g, t_pv[n])
            tcnt += NTI
            gcnt += 2 * NCH
            t_att_pe = t_pv[tcnt - 1]
            if fz is None:
                t_ystore[p] = K.dma("sp", y0T[p], yp[p % 2][:], s_y[p % 2], waits=[t_y])
            else:
                K.dma("sp", agin[p, 0, :, 128:SECL], yp[p % 2][:, 0:half], s_y[p % 2], waits=[t_y])
                t_ystore[p] = K.dma("sp", agin[p, 1, :, 0:SECL], yp[p % 2][:, half - 128:S], s_y[p % 2])
                K.raw("pool", (lambda pi: (lambda e: e.collective_compute(
                    "AllGather", ALU.bypass, replica_groups=fz["groups"],
                    ins=[agin[pi].rearrange("s f t -> (s f) t")],
                    outs=[agout[pi].rearrange("r s f t -> (r s f) t")])))(p),
                    fz["cc"], 1, waits=[t_ystore[p], t_zero])
        if fz is None:
            K._waits("sp", [t_ystore[NP - 1], t_ystore.get(NP - 2)])
            K.flush()
        else:
            fz["bar"] = [t_att_pe, t_exp[tcnt - 1], t_y, t_ystore[NP - 1], t_ystore.get(NP - 2)]


def make_cst():
    ident = np.eye(128, dtype=np.float32)
    s = np.arange(128)[:, None]
    t = np.arange(128)[None, :]
    cmask = np.where(s <= t, 0.0, NEG).astype(np.float32)
    return np.concatenate([ident, cmask], axis=1).astype(NPBF)


def kc_layout(w):
    n = w.shape[1]
    return np.ascontiguousarray(w.reshape(KC, 128, n).transpose(1, 0, 2))


def prep_A(x_b, norm_g0, w_in, b_f, hh, NP=8):
    W = 2048
    wl = []
    for p in range(NP):
        c0 = (hh * 16 + 2 * p) * 64
        cols = np.concatenate([w_in[:, c0:c0 + 128], w_in[:, W + c0:W + c0 + 128],
                               w_in[:, 2 * W + c0:2 * W + c0 + 128], w_in[:, 3 * W + c0:3 * W + c0 + 128]], axis=1)
        wl.append(kc_layout(cols))
    return {
        "x": np.ascontiguousarray(x_b),
        "g0": np.ascontiguousarray(norm_g0[None, :]),
        "wA": np.stack(wl),
        "wf": kc_layout(w_in[:, 4 * W + hh * 16:4 * W + hh * 16 + 16]),
        "bf": np.ascontiguousarray(b_f[hh * 16:hh * 16 + 16][:, None]),
        "cst": make_cst(),
    }


def build_B(NCHK=4, dbg=9):
    NTOK = 128 + 512 * NCHK
    nc = bass.Bass("TRN2", target_bir_lowering=False)
    yT_in = nc.dram_tensor("yT", [128, KC, NTOK], BF16, kind="ExternalInput").ap()
    xr = nc.dram_tensor("xr", [NTOK, D], F32, kind="ExternalInput").ap()
    wo0 = nc.dram_tensor("wo0", [4, 128, KC, 512], F32, kind="ExternalInput").ap()
    wi1 = nc.dram_tensor("wi1", [10, 128, KC, 512], F32, kind="ExternalInput").ap()
    wo1 = nc.dram_tensor("wo1", [4, 128, KC, 512], F32, kind="ExternalInput").ap()
    g1 = nc.dram_tensor("g1", [1, D], F32, kind="ExternalInput").ap()
    gf = nc.dram_tensor("gf", [1, D], F32, kind="ExternalInput").ap()
    snk = nc.dram_tensor("snk", [1, 32], F32, kind="ExternalInput").ap()
    cs = nc.dram_tensor("cs", [128, 2, NTOK], F32, kind="ExternalInput").ap()
    hbias = nc.dram_tensor("hbias", [128, 1], F32, kind="ExternalInput").ap()
    cst2 = nc.dram_tensor("cst2", [128, 1280], BF16, kind="ExternalInput").ap()
    out = nc.dram_tensor("out", [512 * NCHK, D], F32, kind="ExternalOutput").ap()
    wsc = nc.dram_tensor("wsc", [18, 128, KC, 512], BF16).ap()
    K = KB(nc)
    K.dbg = dbg
    try:
        with K.es:
            _build_B_body(K, nc, NCHK, NTOK, yT_in, xr, wo0, wi1, wo1, g1, gf, snk, cs, hbias, cst2, out, wsc)
    except _Cut:
        pass
    return nc


def _build_B_body(K, nc, NCHK, NTOK, yT_in, xr, wo0, wi1, wo1, g1, gf, snk, cs, hbias, cst2, out, wsc, fz=None):
    cst_sb = K.sb("cst_sb", [128, 1280], BF16)
    ident = cst_sb[:, 0:128]
    mrep = [cst_sb[:, 640:1152], cst_sb[:, 128:640]]
    Pm = cst_sb[:, 1152:1280]
    gbc1 = K.sb("gbc1", [128, D], F32)
    gbcf = K.sb("gbcf", [128, D], F32)
    es_bc = K.sb("es_bc", [128, 32], F32)
    hb_sb = K.sb("hb_sb", [128, 1], F32)
    wbuf = [K.sb("wbuf%d" % i, [128, KC, 512], BF16) for i in range(2)]
    yq = K.sb("yq", [128, KC, 512], BF16)
    hy = K.sb("hy", [128, KC, 512], BF16)
    sg1 = K.sb("sg1", [128, KC, 512], BF16)
    xs = K.sb("xs", [128, 4, D], F32)
    kTe = K.sb("kTe", [128, 4, 640], BF16)
    kTo = K.sb("kTo", [128, 4, 640], BF16)
    Vt = K.sb("Vt", [128, 5, 4, 128], BF16)
    cs_sb = K.sb("cs_sb", [128, 2, 512], F32)
    qb = [K.sb("qb%d" % i, [128, 512], BF16) for i in range(2)]
    t1r = [K.sb("t1r%d" % i, [128, 512], F32) for i in range(2)]
    t2r = [K.sb("t2r%d" % i, [128, 512], F32) for i in range(2)]
    pt1 = [K.sb("pt1_%d" % i, [128, 2, 4, 128], BF16) for i in range(2)]
    rt = K.sb("rt", [128, 4, 128], F32)
    t1 = K.sb("t1", [128, 4, 128], F32)
    dsb = [K.sb("dsb%d" % i, [128, 512], F32) for i in range(2)] if F_BATT else None
    pa = K.ps("pa", [128, 4, 512], F32)
    ob1 = K.ps("ob1", [128, 2, 512], F32)
    tpp = K.ps("tpp", [128, 2, 512], F32)
    tp1 = [tpp[:].rearrange("p a b -> p (a b)").bitcast(BF16).rearrange("p (k t) -> p k t", k=KC)]
    st1 = [pa[:, 2 * i:2 * i + 2, :].rearrange("p k (g t) -> p k g t", g=4) for i in range(2)]
    ob1v = [ob1[:, 0, :], ob1[:, 1, :], tpp[:, 0, :], tpp[:, 1, :]]
    NOB = 2
    s_c = K.sem("s_c")
    s_wc = [K.sem("s_wc%d" % i) for i in range(2)]
    s_ws = [K.sem("s_ws%d" % i) for i in range(2)]
    s_wl = [K.sem("s_wl%d" % i) for i in range(2)]
    s_in = K.sem("s_in")
    s_in2 = K.sem("s_in2")
    s_in3 = K.sem("s_in3")
    s_out = K.sem("s_out")

    K.dma("sp", cst_sb[:], cst2, s_c)
    K.dma("sp", gbc1[:], g1.partition_broadcast(128), s_c)
    K.dma("sp", gbcf[:], gf.partition_broadcast(128), s_c)
    K.dma("sp", es_bc[:], snk.partition_broadcast(128), s_c)
    if fz is not None:
        sel = K.sb("sel", [128, 2], F32)
        K.dma("sp", sel[:], fz["sel"], s_c)
    t_c = K.dma("sp", hb_sb[:], hbias, s_c)
    t_es = K.op("act", ACT(es_bc[:], es_bc[:], AF.Exp), waits=[t_c], tok=True)
    K.op("pool", MSET(kTe[:], 0.0))
    K.op("pool", MSET(kTo[:], 0.0))
    t_set = K.op("pool", MSET(Vt[:, :, :, 64:128], 1.0), tok=True)

    t_wst = {}
    if fz is None:
        for t in range(18):
            src = wo0[t] if t < 4 else (wi1[t - 4] if t < 14 else wo1[t - 14])
            tl = None
            for k in range(8):
                tl = K.dma("pool", wbuf[t % 2][:, 2 * k:2 * k + 2, :], src[:, 2 * k:2 * k + 2, :], s_wc[t % 2],
                           waits=[t_wst.get(t - 2)])
            t_wst[t] = K.dma("sp", wsc[t], wbuf[t % 2][:], s_ws[t % 2], waits=[tl])
        t_P = [t_wst[16], t_wst[17]]
    else:
        t_P = [fz["t_P"]]
    K.cut(0, t_P)

    NTn = NormT(K, gbc1, ident, tp1, "n1", nhb=1)
    NTn.junk = NTn.junk

    st = {"ipn": 0, "rn": 0, "ev": {}, "t_pool_rope": {}, "t_pm": {}, "t_d2": {}, "pending": None,
          "tile": 0, "t_exp": {}, "t_pv": {}, "t_norm": {}, "t_lastattn": None, "t_y": None,
          "t_xfree": None, "t_yqfree": None, "t_hyfree": None, "t_out": None}
    steps = []

    st["ypre"] = {}
    st["t_dc"] = {}
    st["t_ln"] = {}

    def y_prefetch(g):
        nt_ = 128 if g == 0 else 512
        t0_ = 0 if g == 0 else (1 + 4 * (g - 1)) * 128
        agout = fz["agout"]
        tl_ = None
        for sec, dstt in ((0, yq), (1, sg1)):
            for r in range(2):
                tl_ = K.dma("sp", dstt[:, r * 8:(r + 1) * 8, 0:nt_],
                            agout[:, r, sec, :, t0_:t0_ + nt_].rearrange("p f t -> f p t"), s_in,
                            waits=[st["t_yqfree"], st["t_yqfree_pe"], st["t_y"], fz["t_cc"]])
        st["ypre"][g] = {"t_dma": tl_, "nt": nt_}

    def y_blend(g):
        e_ = st["ypre"][g]
        nt_ = e_["nt"]
        b1 = K.op("dve", TS(yq[:, :, 0:nt_], yq[:, :, 0:nt_], sel[:, 0:1], ALU.mult), waits=[e_["t_dma"], t_c], tok=True)
        e_["t_y"] = K.op("dve", STT(yq[:, :, 0:nt_], sg1[:, :, 0:nt_], sel[:, 1:2], yq[:, :, 0:nt_],
                                    ALU.mult, ALU.add), waits=[b1], tok=True)

    def group_steps(gi):
        halo = (gi == 0)
        nb = 1 if halo else 4
        NT = nb * 128
        lb0 = 0 if halo else 1 + 4 * (gi - 1)
        tk0 = lb0 * 128
        G = {}

        def f_out0(n):
            def fn(wb, tl):
                if n == 0 and fz is not None:
                    if gi not in st["ypre"]:
                        y_prefetch(gi)
                    if "t_y" not in st["ypre"][gi]:
                        y_blend(gi)
                    G["t_y"] = st["ypre"][gi]["t_y"]
                if n == 0:
                    if fz is None:
                        G["t_y"] = K.dma("sp", yq[:, :, 0:NT], yT_in[:, :, tk0:tk0 + NT], s_in,
                                         waits=[st["t_yqfree"], st["t_yqfree_pe"]])
                    G["t_cs"] = K.dma("sp", cs_sb[:, :, 0:NT], cs[:, :, tk0:tk0 + NT], s_in2, waits=[st.get("t_lastrope")])
                    G["t_x"] = K.dma("sp", xs[:, 0:nb, :], xr[tk0:tk0 + NT, :].rearrange("(i p) d -> p i d", p=128),
                                     s_in3, waits=[st["t_out"], st["t_xfree"]])
                    G["t_add"] = {}
                tok = None
                for blk in range(nb):
                    acc = pa[:, st["ipn"] % 2, :]
                    for kc in range(KC):
                        tok = K.op("pe", MM(acc, yq[:, kc, blk * 128:(blk + 1) * 128], wb[:, kc, :], kc == 0, kc == KC - 1),
                                   waits=[tl, G["t_y"], st["t_lastattn"]] + list(st["ev"].get(st["ipn"] - 2, ())) if kc == 0 else (),
                                   tok=(kc == KC - 1))
                    xv = xs[:, blk, n * 512:(n + 1) * 512]
                    d = K.op("dve", TT(xv, acc, xv, ALU.add), waits=[tok, G["t_x"]], tok=True)
                    st["ev"][st["ipn"]] = (d,)
                    G["t_add"][blk] = d
                    st["ipn"] += 1
                if n == 3:
                    st["t_yqfree_pe"] = tok
                    pre_ = NTn.stats_dve(NTn.stats_act(xs[:, 0, :], [G["t_add"][0], t_c]))
                    for blk in range(nb):
                        sa_ = NTn.stats_act(xs[:, blk + 1, :], [G["t_add"][blk + 1], t_c]) if (blk + 1 < nb and F_BNORM) else None
                        G["t_h"] = NTn.emit(xs[:, blk, :], hy[:, :, blk * 128:(blk + 1) * 128],
                                            waits=[], dst_waits=[st["t_hyfree_pe"]], pre=pre_)
                        if sa_ is not None:
                            pre_ = NTn.stats_dve(sa_)
                        elif blk + 1 < nb:
                            pre_ = NTn.stats_dve(NTn.stats_act(xs[:, blk + 1, :], [G["t_add"][blk + 1], t_c]))
                    st["t_xfree"] = G["t_h"]
                return tok
            return fn

        def rope_first(acc, tok, dst, dst_waits):
            rn = st["rn"]
            a = K.op("act", ACT(qb[rn % 2][:, 0:NT], acc[:, 0:NT], AF.Copy), waits=[tok, st["t_pm"].get(rn - 2)], tok=True)
            d1 = K.op("dve", TT(t1r[rn % 2][:, 0:NT], acc[:, 0:NT], cs_sb[:, 0, 0:NT], ALU.mult),
                      waits=[tok, a, st["t_pool_rope"].get(rn - 2), G["t_cs"]], tok=True)
            st["pending"] = (rn, a, d1, dst, dst_waits)
            st["rn"] += 1
            return (a, d1)

        def rope_second():
            if st["pending"] is None:
                return
            rn, a, d1, dst, dst_waits = st["pending"]
            st["pending"] = None
            pp = pa[:, 2 + rn % 2, :]
            pm = K.op("pe", MM(pp[:, 0:NT], Pm, qb[rn % 2][:, 0:NT], True, True), waits=[a, st["t_d2"].get(rn - 2)], tok=True)
            st["t_pm"][rn] = pm
            d2 = K.op("dve", TT(t2r[rn % 2][:, 0:NT], pp[:, 0:NT], cs_sb[:, 1, 0:NT], ALU.mult),
                      waits=[pm, st["t_pool_rope"].get(rn - 2)], tok=True)
            st["t_d2"][rn] = d2
            pl = None
            for (r0_, r1_, dap) in dst:
                pl = K.op("dve", TT(dap, t1r[rn % 2][r0_:r1_, 0:NT], t2r[rn % 2][r0_:r1_, 0:NT], ALU.add),
                          waits=[d1, d2] + list(dst_waits), tok=True)
            st["t_pool_rope"][rn] = pl
            st["t_lastrope"] = pl

        def f_in(wt):
            def fn(wb, tl):
                tok = None
                if halo and wt == 4 and fz is not None and NCHK >= 1:
                    y_prefetch(1)
                if wt == 9:
                    for blk in range(nb):
                        acc = pa[:, st["ipn"] % 2, 0:256]
                        for kc in range(KC):
                            tok = K.op("pe", MM(acc, hy[:, kc, blk * 128:(blk + 1) * 128], wb[:, kc, 0:256], kc == 0, kc == KC - 1),
                                       waits=[tl, G["t_h"]] + list(st["ev"].get(st["ipn"] - 2, ())) if kc == 0 else (),
                                       tok=(kc == KC - 1))
                        slot = 0 if halo else 1 + blk
                        a = K.op("act", ACT(Vt[:, slot, :, 0:64], acc.rearrange("p (h d) -> p h d", h=4), AF.Copy),
                                 waits=[tok, t_set, st["t_lastattn"], st.get("t_ring")], tok=True)
                        st["ev"][st["ipn"]] = (a,)
                        st["ipn"] += 1
                    G["t_v"] = a
                    rope_second()
                    if halo and fz is not None and 1 in st["ypre"]:
                        y_blend(1)
                    return tok
                for fl in range(4):
                    acc = pa[:, st["ipn"] % 2, :]
                    for kc in range(KC):
                        tok = K.op("pe", MM(acc[:, 0:NT], wb[:, kc, fl * 128:(fl + 1) * 128], hy[:, kc, 0:NT], kc == 0, kc == KC - 1),
                                   waits=[tl, G["t_h"]] + list(st["ev"].get(st["ipn"] - 2, ())) if kc == 0 else (),
                                   tok=(kc == KC - 1))
                    rope_second()
                    if wt < 4:
                        f = 4 * wt + fl
                        evt = rope_first(acc, tok, [(0, 128, yq[:, f, 0:NT])], [st["t_yqfree_pe"]])
                    elif wt == 4:
                        k0 = 0 if halo else 128
                        evt = rope_first(acc, tok, [(0, 64, kTe[0:64, fl, k0:k0 + NT]), (64, 128, kTo[64:128, fl, k0:k0 + NT])],
                                         [st["t_lastattn"], st.get("t_ring"), t_set])
                    else:
                        f = 4 * (wt - 5) + fl
                        a = K.op("act", ACT(sg1[:, f, 0:NT], acc[:, 0:NT], AF.Silu), waits=[tok, st.get("t_y1"), G["t_y"]], tok=True)
                        evt = (a,)
                    st["ev"][st["ipn"]] = evt
                    st["ipn"] += 1
                return tok
            return fn

        def attention():
            rope_second()
            q_ready = [st["t_lastrope"], G["t_v"]]
            tiles = [(blk, h, hf) for blk in range(4) for h in range(4) for hf in range(2)]

            def e_qk(blk, h, hf, n):
                buf = n % 2
                tok = None
                for kbi in range(2):
                    kcol = blk * 128 + kbi * 128
                    K.op("pe", MM(st1[buf][:, kbi].rearrange("p g t -> p (g t)"), ident, mrep[kbi], True, False),
                         waits=[st["t_exp"].get(n - 2)] + q_ready + list(st["ev"].get(st["ipn"] - 1, ())) + list(st["ev"].get(st["ipn"] - 2, ()))
                         if kbi == 0 else ())
                    for gg in range(4):
                        hq = 8 * h + 4 * hf + gg
                        f = hq // 2
                        kz = kTe if hq % 2 == 0 else kTo
                        tok = K.op("pe", MM(st1[buf][:, kbi, gg, :], kz[:, h, kcol:kcol + 128],
                                            yq[:, f, blk * 128:(blk + 1) * 128], False, True),
                                   tok=(kbi == 1 and gg == 3))
                return tok

            def e_exp(blk, h, hf, n, tq):
                buf = n % 2
                w = [tq, st["t_pv"].get(n - 2)]
                if gi == 1 and blk == 0:
                    K.op("act", ACT(pt1[buf][:, 0], st1[buf][:, 0], AF.Exp, scale=0.125, bias=hb_sb[:, 0:1]), waits=w + [t_c])
                    return K.op("act", ACT(pt1[buf][:, 1], st1[buf][:, 1], AF.Exp, scale=0.125), tok=True)
                return K.op("act", ACT(pt1[buf][:], st1[buf], AF.Exp, scale=0.125), waits=w, tok=True)

            def e_pv(blk, h, hf, n, te):
                buf = n % 2
                tok = None
                for gg in range(4):
                    for kbi in range(2):
                        tok = K.op("pe", MM(ob1v[n % NOB][:, gg * 128:(gg + 1) * 128], Vt[:, blk + kbi, h, :], pt1[buf][:, kbi, gg, :],
                                            kbi == 0, kbi == 1),
                                   waits=[te, st["t_norm"].get(n - NOB)] if (gg == 0 and kbi == 0) else (),
                                   tok=(gg == 3 and kbi == 1))
                return tok

            def e_dcopy(n, tp_):
                st["t_dc"][n] = K.op("dve", CP(dsb[n % 2][64:128, :], ob1v[n % NOB][64:128, :]),
                                     waits=[tp_, st["t_ln"].get(n - 2)], tok=True)

            def e_norm(blk, h, hf, n, tp_):
                o = ob1v[n % NOB]
                cols = slice(blk * 128, (blk + 1) * 128)
                a = None
                dsrc = dsb[n % 2] if F_BATT else o
                if F_BATT:
                    tp_ = st["t_dc"][n]
                for gg in range(4):
                    hq = 8 * h + 4 * hf + gg
                    a = K.op("act", ACT(rt[64:128, gg, :], dsrc[64:128, gg * 128:(gg + 1) * 128], AF.Ln,
                                        bias=es_bc[64:128, hq:hq + 1], scale=1.0),
                             waits=[tp_, t_es, st["t_y"]] if gg == 0 else (), tok=(gg == 3))
                st["t_ln"][n] = a
                a = K.op("act", ACT(rt[64:128].rearrange("p g t -> p (g t)"), rt[64:128].rearrange("p g t -> p (g t)"),
                                    AF.Exp, scale=-1.0), waits=[a], tok=True)
                K.op("dve", TT(t1[0:64].rearrange("p g t -> p (g t)"), o[0:64, :],
                               rt[64:128].rearrange("p g t -> p (g t)"), ALU.mult), waits=[a, st["t_y"]])
                d = K.op("dve", TT(t1[64:128].rearrange("p g t -> p (g t)"), o[0:64, :],
                                   rt[64:128].rearrange("p g t -> p (g t)"), ALU.mult), tok=True)
                st["t_norm"][n] = d
                K.cut(203, [d])
                f0 = (8 * h + 4 * hf) // 2
                d = K.op("dve", TT(hy[0:64, f0:f0 + 2, cols], t1[0:64, 0:4:2, :], sg1[0:64, f0:f0 + 2, cols], ALU.mult),
                         waits=[d], tok=True)
                d = K.op("dve", TT(hy[64:128, f0:f0 + 2, cols], t1[64:128, 1:4:2, :], sg1[64:128, f0:f0 + 2, cols], ALU.mult),
                         tok=True)
                st["t_y"] = d

            n0 = st["tile"]
            tq = {0: e_qk(*tiles[0], n0)}
            for i, (blk, h, hf) in enumerate(tiles):
                n = n0 + i
                if i + 1 < len(tiles):
                    tq[i + 1] = e_qk(*tiles[i + 1], n + 1)
                st["t_exp"][n] = e_exp(blk, h, hf, n, tq[i])
                K.cut(200, [st["t_exp"][n]])
                st["t_pv"][n] = e_pv(blk, h, hf, n, st["t_exp"][n])
                K.cut(201, [st["t_pv"][n]])
                if not F_BATT:
                    e_norm(blk, h, hf, n, st["t_pv"][n])
                else:
                    e_dcopy(n, st["t_pv"][n])
                    if i > 0:
                        e_norm(*tiles[i - 1], n - 1, st["t_pv"][n - 1])
                    if i == len(tiles) - 1:
                        e_norm(blk, h, hf, n, st["t_pv"][n])
                K.cut(204, [st["t_y"]])
            st["tile"] += len(tiles)
            K.cut(205, [st["t_y"]])
            st["t_lastattn"] = st["t_pv"][st["tile"] - 1]
            st["t_yqfree"] = st["t_lastattn"]
            st["t_y1"] = st["t_y"]
            K.op("pool", CP(kTe[:, :, 0:128], kTe[:, :, 512:640]), waits=[st["t_lastattn"]])
            K.op("pool", CP(kTo[:, :, 0:128], kTo[:, :, 512:640]))
            r2_ = K.op("pool", CP(Vt[:, 0, :, 0:64], Vt[:, 4, :, 0:64]), tok=True)
            st["t_ring"] = r2_
            K.cut(206, [r2_])

        def f_out1(n):
            def fn(wb, tl):
                if n == 0:
                    attention()
                    G["t_add2"] = {}
                    if fz is not None and gi + 1 <= NCHK:
                        y_prefetch(gi + 1)
                tok = None
                for blk in range(4):
                    acc = pa[:, st["ipn"] % 2, :]
                    for kc in range(KC):
                        tok = K.op("pe", MM(acc, hy[:, kc, blk * 128:(blk + 1) * 128], wb[:, kc, :], kc == 0, kc == KC - 1),
                                   waits=[tl, st["t_y"], st["t_exp"][st["tile"] - 1], st["t_exp"][st["tile"] - 2]]
                                   + list(st["ev"].get(st["ipn"] - 2, ())) if kc == 0 else (),
                                   tok=(kc == KC - 1))
                    xv = xs[:, blk, n * 512:(n + 1) * 512]
                    d = K.op("dve", TT(xv, acc, xv, ALU.add), waits=[tok], tok=True)
                    st["ev"][st["ipn"]] = (d,)
                    G["t_add2"][blk] = d
                    st["ipn"] += 1
                if n == 2 and fz is not None and (gi + 1) in st["ypre"]:
                    y_blend(gi + 1)
                if n == 3:
                    st["t_hyfree_pe"] = tok
                    t_o = None
                    sas = [NTn.stats_act(xs[:, blk, :], [G["t_add2"][blk]]) for blk in range(4)]
                    for blk in range(4):
                        t, col = NTn.stats_dve(sas[blk])
                        d = K.op("dve", STT(xs[:, blk, :], xs[:, blk, :], NTn.rstd[:, col:col + 1], gbcf[:], ALU.mult, ALU.mult),
                                 waits=[t], tok=True)
                        r0 = (lb0 - 1 + blk) * 128
                        t_o = K.dma("sp", out[r0:r0 + 128, :], xs[:, blk, :], s_out, waits=[d])
                    st["t_out"] = t_o
                return tok
            return fn

        for n in range(4):
            steps.append((n, f_out0(n)))
        for wt in ((4, 9) if halo else range(10)):
            steps.append((4 + wt, f_in(wt)))
        if not halo:
            for n in range(4):
                steps.append((14 + n, f_out1(n)))

    st["t_hyfree_pe"] = None
    st["t_yqfree_pe"] = None
    for gi in range(1 + NCHK):
        group_steps(gi)

    t_load = {}
    t_use = {}

    def issue(i):
        t_load[i] = K.dma("sp", wbuf[i % 2][:], wsc[steps[i][0]], s_wl[i % 2], waits=t_P + [t_use.get(i - 2)])

    issue(0)
    for i, (widx, fn) in enumerate(steps):
        if i + 1 < len(steps):
            issue(i + 1)
        t_use[i] = fn(wbuf[i % 2], t_load[i])
        K.cut(100 + i, [t_use[i]])
    K._waits("sp", [st["t_out"]])
    K.flush()


def build_fused(S=4096, NP=8, groups=None):
    NB = S // 128
    NCH = S // 512
    half = S // 2
    NCHK = half // 512
    NTOK = half + 128
    SECL = NTOK
    if groups is None:
        groups = [[0, 1], [2, 3], [4, 5], [6, 7]]
    nc = bass.Bass("TRN2", target_bir_lowering=False)
    di = lambda name, shape, dt: nc.dram_tensor(name, shape, dt, kind="ExternalInput").ap()
    x = di("x", [S, D], F32)
    g0 = di("g0", [1, D], F32)
    wA = di("wA", [NP, 128, KC, 512], F32)
    wf = di("wf", [128, KC, 16], F32)
    bfv = di("bf", [16, 1], F32)
    cst = di("cst", [128, 256], BF16)
    xr = di("xr", [NTOK, D], F32)
    wo0 = di("wo0", [4, 128, KC, 512], F32)
    wi1 = di("wi1", [10, 128, KC, 512], F32)
    wo1 = di("wo1", [4, 128, KC, 512], F32)
    g1 = di("g1", [1, D], F32)
    gf = di("gf", [1, D], F32)
    snk = di("snk", [1, 32], F32)
    cs = di("cs", [128, 2, NTOK], F32)
    hbias = di("hbias", [128, 1], F32)
    cst2 = di("cst2", [128, 1280], BF16)
    sel_d = di("sel", [128, 2], F32)
    out = nc.dram_tensor("out", [half, D], F32, kind="ExternalOutput").ap()
    hT_d = nc.dram_tensor("hT_d", [NCH, 128, KC, 512], BF16).ap()
    caug = nc.dram_tensor("caug", [16, 2, 3, S], BF16).ap()
    wsc = nc.dram_tensor("wsc", [18, 128, KC, 512], BF16).ap()
    agin = nc.dram_tensor("agin", [NP, 2, 128, SECL], BF16).ap()
    agout = nc.dram_tensor("agout", [NP, 2, 2, 128, SECL], BF16).ap()

    K = KB(nc)
    with K.es:
        s_wp = K.sem("s_wp")
        cc = K.sem("cc")
        fz = {"agin": agin, "agout": agout, "groups": groups, "cc": cc, "sel": sel_d}
        per = [(18 * p) // NP for p in range(NP + 1)]

        def emit_P(p):
            for t in range(per[p], per[p + 1]):
                src = wo0[t] if t < 4 else (wi1[t - 4] if t < 14 else wo1[t - 14])
                for k in range(8):
                    fz["t_P"] = K.dma("pool", wsc[t][:, 2 * k:2 * k + 2, :], src[:, 2 * k:2 * k + 2, :], s_wp)

        fz["emit_P"] = emit_P
        with ExitStack() as esA:
            K.cur = esA
            K.prefix = "A_"
            _build_A_body(K, nc, S, NP, NB, NCH, x, g0, wA, wf, bfv, cst, None, hT_d, caug, 9, fz=fz)
            K.flush()
        fz["t_cc"] = (cc, cc.n)
        K.barrier(fz["bar"])
        with ExitStack() as esB:
            K.cur = esB
            K.prefix = "B_"
            _build_B_body(K, nc, NCHK, NTOK, None, xr, wo0, wi1, wo1, g1, gf, snk, cs, hbias, cst2, out, wsc, fz=fz)
        K.cur = None
    return nc


def make_cst2():
    ident = np.eye(128, dtype=np.float32)
    s = np.arange(128)[:, None]
    t = np.arange(128)[None, :]
    mcur = np.where(s <= t, 0.0, NEG).astype(np.float32)
    mprev = np.where(s > t, 0.0, NEG).astype(np.float32)
    Pm = np.zeros((128, 128), np.float32)
    for do in range(128):
        d = do % 64
        if d < 16:
            di = do + 8 if d < 8 else do - 8
            Pm[di, do] = 1.0
    return np.concatenate([ident, np.tile(mcur, (1, 4)), np.tile(mprev, (1, 4)), Pm], axis=1).astype(NPBF)


def rope_tables(pos):
    half = 8
    inv_freq = (np.float32(500000.0) ** (-np.arange(half, dtype=np.float32) / np.float32(half))).astype(np.float32)
    ang = (pos.astype(np.float32)[:, None] * inv_freq[None, :]).astype(np.float32)
    cos = np.cos(ang).astype(np.float32).T
    sin = np.sin(ang).astype(np.float32).T
    n = pos.shape[0]
    C = np.ones((128, n), np.float32)
    Sg = np.zeros((128, n), np.float32)
    for base in (0, 64):
        C[base:base + 8] = cos
        C[base + 8:base + 16] = cos
        Sg[base:base + 8] = -sin
        Sg[base + 8:base + 16] = sin
    return np.ascontiguousarray(np.stack([C, Sg], axis=1))


def tile_layout(w):
    n = w.shape[1] // 512
    return np.ascontiguousarray(w.reshape(KC, 128, n, 512).transpose(2, 1, 0, 3))


def prep_B_weights(w_out0, w_in1, w_out1):
    WQ, WK = 2048, 256
    q = w_in1[:, :WQ]
    k = w_in1[:, WQ:WQ + WK]
    v = w_in1[:, WQ + WK:WQ + 2 * WK]
    g = w_in1[:, WQ + 2 * WK:]
    kdup = np.concatenate([np.concatenate([k[:, h * 64:(h + 1) * 64]] * 2, axis=1) for h in range(4)], axis=1)
    vpad = np.concatenate([v, np.zeros((2048, 256), np.float32)], axis=1)
    wi = np.concatenate([q, kdup, g, vpad], axis=1)
    return {"wo0": tile_layout(w_out0), "wi1": tile_layout(wi), "wo1": tile_layout(w_out1)}


def prep_B(yT_full, x_rows, pos, hh, wts, g1, gf, sinks):
    ntok = x_rows.shape[0]
    d = dict(wts)
    d["yT"] = np.ascontiguousarray(yT_full.reshape(KC, 128, ntok).transpose(1, 0, 2))
    d["xr"] = np.ascontiguousarray(x_rows)
    d["g1"] = np.ascontiguousarray(g1[None, :])
    d["gf"] = np.ascontiguousarray(gf[None, :])
    d["snk"] = np.ascontiguousarray(sinks[None, :])
    d["cs"] = rope_tables(pos)
    d["hbias"] = np.full((128, 1), NEG if hh == 0 else 0.0, np.float32)
    d["cst2"] = make_cst2()
    return d


def kernel(x, norm_g, fox_w_in, fox_b_f, fox_w_out, swa_w_in, swa_sinks, swa_w_out, final_g):
    x = np.asarray(x, dtype=np.float32)
    norm_g = np.asarray(norm_g, dtype=np.float32)
    fox_w_in = np.asarray(fox_w_in, dtype=np.float32)
    fox_b_f = np.asarray(fox_b_f, dtype=np.float32)
    fox_w_out = np.asarray(fox_w_out, dtype=np.float32)
    swa_w_in = np.asarray(swa_w_in, dtype=np.float32)
    swa_sinks = np.asarray(swa_sinks, dtype=np.float32)
    swa_w_out = np.asarray(swa_w_out, dtype=np.float32)
    final_g = np.asarray(final_g, dtype=np.float32)
    B, S, _ = x.shape
    ncores = 8
    half = S // 2
    nc = build_fused(S, 8)
    wA_parts = [prep_A(x[0], norm_g[0], fox_w_in[0], fox_b_f[0], hh) for hh in range(2)]
    for wp in wA_parts:
        wp.pop("x")
    wtsB = prep_B_weights(fox_w_out[0], swa_w_in[0], swa_w_out[0])
    maps = []
    for c in range(ncores):
        b, hh = c // 2, c % 2
        maps.append(prep_fused(x[b], hh, S, wA_parts, wtsB, norm_g, final_g, swa_sinks[0]))
    res = run_bass_kernel_spmd(nc, maps, core_ids=list(range(ncores)))
    out = np.zeros((B, S, D), np.float32)
    for c in range(ncores):
        b, hh = c // 2, c % 2
        out[b, hh * half:(hh + 1) * half] = np.asarray(res.results[c]["out"]).reshape(half, D)
    return out


def prep_fused(x_b, hh, S, wA_parts, wtsB, norm_g, final_g, sinks):
    half = S // 2
    ntok = half + 128
    t0 = hh * half - 128
    toks = np.arange(t0, t0 + ntok)
    valid = toks >= 0
    xr = np.zeros((ntok, D), np.float32)
    xr[valid] = x_b[toks[valid]]
    m = dict(wA_parts[hh])
    m["x"] = np.ascontiguousarray(x_b)
    m.update(wtsB)
    m["xr"] = xr
    m["g1"] = np.ascontiguousarray(norm_g[1][None, :])
    m["gf"] = np.ascontiguousarray(final_g[None, :])
    m["snk"] = np.ascontiguousarray(sinks[None, :])
    m["cs"] = rope_tables(toks.astype(np.float32))
    m["hbias"] = np.full((128, 1), NEG if hh == 0 else 0.0, np.float32)
    m["cst2"] = make_cst2()
    sel = np.zeros((128, 2), np.float32)
    sel[:, hh] = 1.0
    m["sel"] = sel
    return m
```

```python
import numpy as np
import ml_dtypes
from contextlib import ExitStack
import concourse.bass as bass
import concourse.mybir as mybir
from concourse.bass_utils import run_bass_kernel_spmd

F32 = mybir.dt.float32
BF16 = mybir.dt.bfloat16
AF = mybir.ActivationFunctionType
ALU = mybir.AluOpType
NPBF = ml_dtypes.bfloat16

F_BATT = True
F_BNORM = True
D = 2048
KC = 16
EPS = 1e-6
NEG = -30000.0


class Sem:
    def __init__(self, h, name):
        self.h = h
        self.name = name
        self.n = 0


class _Cut(Exception):
    pass


class KB:
    ENG = ("pe", "act", "dve", "pool", "sp")
    dbg = None

    def cut(self, level, waits):
        if self.dbg == level:
            self._waits("sp", waits)
            self.flush()
            raise _Cut()

    def __init__(self, nc):
        self.nc = nc
        self.es = ExitStack()
        self.q = {e: [] for e in self.ENG}
        self.waited = {e: {} for e in self.ENG}
        self.esem = {}
        for e in ("pe", "act", "dve", "pool"):
            self.esem[e] = self.sem("e_" + e)

    prefix = ""
    cur = None

    def sem(self, name):
        name = self.prefix + name
        return Sem(self.es.enter_context(self.nc.semaphore(name)), name)

    def sb(self, name, shape, dt):
        return (self.cur or self.es).enter_context(self.nc.sbuf_tensor(self.prefix + name, shape, dt))

    def ps(self, name, shape, dt):
        return (self.cur or self.es).enter_context(self.nc.psum_tensor(self.prefix + name, shape, dt))

    def raw(self, eng, fn, sem, n, waits=()):
        self._waits(eng, waits)
        sem.n += n
        self.q[eng].append((1, fn, sem.h, n))
        return (sem, sem.n)

    def barrier(self, toks):
        for eng in self.ENG:
            self._waits(eng, toks)

    def _waits(self, eng, waits):
        for w in waits:
            if w is None:
                continue
            s, v = w
            if v <= 0 or self.waited[eng].get(s.name, 0) >= v:
                continue
            self.waited[eng][s.name] = v
            self.q[eng].append((0, s.h, v))

    def op(self, eng, fn, waits=(), tok=False):
        self._waits(eng, waits)
        if tok:
            s = self.esem[eng]
            s.n += 1
            self.q[eng].append((1, fn, s.h, 1))
            return (s, s.n)
        self.q[eng].append((1, fn, None, 0))
        return None

    def dma(self, eng, out, in_, sem, waits=()):
        self._waits(eng, waits)
        sem.n += 16
        self.q[eng].append((1, (lambda e: e.dma_start(out=out, in_=in_)), sem.h, 16))
        return (sem, sem.n)

    def flush(self):
        with self.nc.Block() as block:
            dec = {"pe": block.tensor, "act": block.scalar, "dve": block.vector,
                   "pool": block.gpsimd, "sp": block.sync}
            for eng in self.ENG:
                items = self.q[eng]
                self.q[eng] = []
                if not items:
                    continue

                def body(e, items=items):
                    for it in items:
                        if it[0] == 0:
                            e.wait_ge(it[1], it[2])
                        else:
                            ins = it[1](e)
                            if it[2] is not None:
                                ins.then_inc(it[2], it[3])

                dec[eng](body)


def MM(out, lhsT, rhs, start, stop):
    return lambda e: e.matmul(out, lhsT=lhsT, rhs=rhs, start=start, stop=stop, skip_group_check=True)


def TR(out, in_, ident):
    return lambda e: e.transpose(out, in_, ident)


def ACT(out, in_, func, **kw):
    return lambda e: e.activation(out=out, in_=in_, func=func, **kw)


def TT(out, in0, in1, op):
    return lambda e: e.tensor_tensor(out=out, in0=in0, in1=in1, op=op)


def TS(out, in0, s1, op0, s2=None, op1=None):
    if op1 is None:
        return lambda e: e.tensor_scalar(out=out, in0=in0, scalar1=s1, scalar2=None, op0=op0)
    return lambda e: e.tensor_scalar(out=out, in0=in0, scalar1=s1, scalar2=s2, op0=op0, op1=op1)


def STT(out, in0, scalar, in1, op0, op1):
    return lambda e: e.scalar_tensor_tensor(out=out, in0=in0, scalar=scalar, in1=in1, op0=op0, op1=op1)


def CP(out, in_):
    return lambda e: e.tensor_copy(out=out, in_=in_)


def RCP(out, in_):
    return lambda e: e.reciprocal(out=out, in_=in_)


def MSET(ap, c):
    return lambda e: e.memset(ap, c)


class NormT:
    def __init__(self, K, gbc, ident, tp2, name, nhb=2):
        self.K = K
        self.gbc = gbc
        self.ident = ident
        self.tp2 = tp2
        self.junk = K.sb(name + "_junk", [128, D], BF16)
        self.hb = [K.sb(name + "_hb%d" % i, [128, D], BF16) for i in range(nhb)]
        self.ss = K.sb(name + "_ss", [128, 64], F32)
        self.std = K.sb(name + "_std", [128, 64], F32)
        self.rstd = K.sb(name + "_rstd", [128, 64], F32)
        self.n = 0
        self.sc = 0
        self.t_tr = {}
        self.t_cp = {}
        self.t_stt = {}

    def stats_act(self, xs, waits):
        K = self.K
        col = self.sc % 64
        self.sc += 1
        t = K.op("act", ACT(self.junk[:], xs, AF.Square, accum_out=self.ss[:, col:col + 1]),
                 waits=waits, tok=True)
        t = K.op("act", ACT(self.std[:, col:col + 1], self.ss[:, col:col + 1], AF.Sqrt,
                            scale=1.0 / D, bias=EPS), waits=[t], tok=True)
        return t, col

    def stats_dve(self, tc):
        t, col = tc
        t = self.K.op("dve", RCP(self.rstd[:, col:col + 1], self.std[:, col:col + 1]), waits=[t], tok=True)
        return t, col

    def stats(self, xs, waits):
        return self.stats_dve(self.stats_act(xs, waits))

    def emit(self, xs, dst, waits, dst_waits=(), pre=None):
        K = self.K
        n = self.n
        t, col = pre if pre is not None else self.stats(xs, waits)
        K.cut(-4, [t])
        nh = len(self.hb)
        ntp = len(self.tp2)
        hb = self.hb[n % nh]
        tp = self.tp2[n % ntp]
        t_stt = K.op("dve", STT(hb[:], xs, self.rstd[:, col:col + 1], self.gbc[:],
                                ALU.mult, ALU.mult),
                     waits=[t, self.t_tr.get(n - nh)], tok=True)
        self.t_stt[n] = t_stt
        K.cut(-3, [t_stt])
        tok = None
        for kc in range(KC):
            tok = K.op("pe", TR(tp[:, kc, :], hb[:, kc * 128:(kc + 1) * 128], self.ident),
                       waits=[t_stt, self.t_cp.get(n - ntp)] if kc == 0 else (), tok=(kc == KC - 1))
        self.t_tr[n] = tok
        t_cp = K.op("act", ACT(dst, tp[:], AF.Copy), waits=[tok] + list(dst_waits), tok=True)
        self.t_cp[n] = t_cp
        K.cut(-2, [t_cp])
        self.n += 1
        return t_cp


def build_A(S=4096, NP=8, dbg=9):
    NB = S // 128
    NCH = S // 512
    NH = 2 * NP
    nc = bass.Bass("TRN2", target_bir_lowering=False)
    x = nc.dram_tensor("x", [S, D], F32, kind="ExternalInput").ap()
    g0 = nc.dram_tensor("g0", [1, D], F32, kind="ExternalInput").ap()
    wA = nc.dram_tensor("wA", [NP, 128, KC, 512], F32, kind="ExternalInput").ap()
    wf = nc.dram_tensor("wf", [128, KC, 16], F32, kind="ExternalInput").ap()
    bfv = nc.dram_tensor("bf", [16, 1], F32, kind="ExternalInput").ap()
    cst = nc.dram_tensor("cst", [128, 256], BF16, kind="ExternalInput").ap()
    y0T = nc.dram_tensor("y0T", [NP, 128, S], BF16, kind="ExternalOutput").ap()
    hT_d = nc.dram_tensor("hT_d", [NCH, 128, KC, 512], BF16).ap()
    caug = nc.dram_tensor("caug", [16, 2, 3, S], BF16).ap()

    K = KB(nc)
    K.dbg = dbg
    try:
      with K.es:
        _build_A_body(K, nc, S, NP, NB, NCH, x, g0, wA, wf, bfv, cst, y0T, hT_d, caug, dbg)
    except _Cut:
        pass
    return nc


def _build_A_body(K, nc, S, NP, NB, NCH, x, g0, wA, wf, bfv, cst, y0T, hT_d, caug, dbg, fz=None):
    if True:
        cst_sb = K.sb("cst_sb", [128, 256], BF16)
        ident = cst_sb[:, 0:128]
        cmask = cst_sb[:, 128:256]
        gbc = K.sb("gbc", [128, D], F32)
        NXB = 3
        xbuf = [K.sb("xbuf%d" % i, [128, D], F32) for i in range(NXB)]
        hbuf = [K.sb("hbuf%d" % i, [128, KC, 512], BF16) for i in range(2)]
        wfb = K.sb("wfb", [128, KC, 16], BF16)
        nb = K.sb("nb", [16, 2], F32)
        ef = K.sb("ef", [16, 512], F32)
        lsp = K.sb("lsp", [16, 512], F32)
        ones16 = K.sb("ones16", [16, 512], F32)
        Ec = [K.sb("Ec%d" % i, [16, 512], F32) for i in range(2)]
        e8 = K.sb("e8", [16, 512], F32)
        r1 = K.sb("r1", [16, 512], F32)
        r2 = K.sb("r2", [16, 512], F32)
        TKt = K.sb("TKt", [16, 3, 512], BF16)
        TQt = K.sb("TQt", [16, 3, 512], BF16)
        wbuf = [K.sb("wbuf%d" % i, [128, KC, 512], BF16) for i in range(2)]
        QA = K.sb("QA", [128, S], BF16)
        QB = K.sb("QB", [128, S], BF16)
        KA = K.sb("KA", [128, S], BF16)
        KBt = K.sb("KBt", [128, S], BF16)
        VA = K.sb("VA", [128, NB, 128], BF16)
        VB = K.sb("VB", [128, NB, 128], BF16)
        sg = K.sb("sg", [128, S], BF16)
        NSB = 3
        pt = [K.sb("pt%d" % i, [128, 2, 512], BF16) for i in range(NSB)]
        rt = K.sb("rt", [128, 512], F32)
        t1 = K.sb("t1", [128, 512], F32)
        yp = [K.sb("yp%d" % i, [128, S], BF16) for i in range(1)] * 2
        st = K.ps("st", [128, 2, 2, 512], F32)
        ob = K.ps("ob", [128, 2, 512], F32)
        ip = K.ps("ip", [128, 2, 512], F32)
        tp2 = [st[:, i].rearrange("p a b -> p (a b)").bitcast(BF16).rearrange("p (k t) -> p k t", k=KC)
               for i in range(2)]
        fps = ob[0:16, 0, :]
        stv = [st[:, 0], st[:, 1], ip[:]]
        s_c = K.sem("s_c")
        s_c2 = K.sem("s_c2")
        s_x = [K.sem("s_x%d" % i) for i in range(3)]
        s_hst = [K.sem("s_hst%d" % i) for i in range(2)]
        s_cst = K.sem("s_cst")
        s_w = [K.sem("s_w%d" % i) for i in range(2)]
        s_h = [K.sem("s_h%d" % i) for i in range(2)]
        s_aug = K.sem("s_aug")
        s_y = [K.sem("s_y%d" % i) for i in range(2)]

        t_c0 = K.dma("sp", cst_sb[:], cst, s_c)
        K.dma("sp", gbc[:], g0.partition_broadcast(128), s_c)
        t_c = K.dma("sp", nb[:, 0:1], bfv, s_c)
        t_c2 = K.dma("pool", wfb[:], wf, s_c2)
        t_nb = K.op("dve", TS(nb[:, 1:2], nb[:, 0:1], -1.0, ALU.mult), waits=[t_c], tok=True)
        K.op("dve", MSET(ones16[:], 1.0))
        K.op("pool", MSET(QA[64:70, :], 1.0))
        K.op("pool", MSET(KA[64:70, :], 1.0))
        K.op("pool", MSET(QB[0:64, :], 0.0))
        K.op("pool", MSET(KBt[0:64, :], 0.0))
        K.op("pool", MSET(VA[:, :, 64:128], 1.0))
        K.op("pool", MSET(VB[:, :, 0:64], 1.0), tok=True)
        K.op("pool", MSET(QB[0:6, :], 1.0), waits=[(K.esem["pool"], K.esem["pool"].n)])
        t_set = K.op("pool", MSET(KBt[0:6, :], 1.0), tok=True)
        if fz is not None:
            half = S // 2
            SECL = half + 128
            agin, agout = fz["agin"], fz["agout"]
            zt = K.sb("zt", [128, NP, 128], BF16)
            s_z = K.sem("s_z")
            tz = K.op("dve", MSET(zt[:], 0.0), tok=True)
        K.cut(-5, [t_set, t_nb, t_c0, t_c2])

        NT = NormT(K, gbc, ident[:, :] if False else ident, tp2, "n0")
        t_hst = {}
        t_fexp = {}
        t_cstore = {}
        t_scan = {}
        t_fmm = {}
        pre = {}
        t_xld = {}

        def x_load(b):
            if b < NB:
                t_xld[b] = K.dma("sp", xbuf[b % NXB][:], x[b * 128:(b + 1) * 128, :], s_x[b % NXB],
                                 waits=[NT.t_stt.get(b - NXB)])

        def x_stats_act(b):
            return NT.stats_act(xbuf[b % NXB][:], [t_xld[b], t_c]) if b < NB else None

        x_load(0)
        x_load(1)
        pre[0] = NT.stats_dve(x_stats_act(0))
        for b in range(NB):
            c, bi = b // 4, b % 4
            xs = xbuf[b % NXB]
            x_load(b + 2)
            sa = x_stats_act(b + 1)
            t_cp = NT.emit(xs[:], hbuf[c % 2][:, :, bi * 128:(bi + 1) * 128], waits=[],
                           dst_waits=[t_hst.get(c - 2), t_fmm.get(c - 2)] if bi == 0 else (), pre=pre[b])
            if sa is not None:
                pre[b + 1] = NT.stats_dve(sa)
            if bi != 3:
                continue
            t_hst[c] = K.dma("sp", hT_d[c], hbuf[c % 2][:], s_hst[c % 2], waits=[t_cp])
            tok = None
            for kc in range(KC):
                tok = K.op("pe", MM(fps, wfb[:, kc, :], hbuf[c % 2][:, kc, :], kc == 0, kc == KC - 1),
                           waits=[t_cp, t_c2, t_fexp.get(c - 1)] if kc == 0 else (), tok=(kc == KC - 1))
            t_fmm[c] = tok
            K.cut(-1, [tok, t_hst[c]])
            t_fexp[c] = K.op("act", ACT(ef[:], fps, AF.Exp, scale=-1.0, bias=nb[:, 1:2]),
                             waits=[tok, t_nb, t_scan.get(c - 1)], tok=True)
            K.cut(10, [t_fexp[c]])
            t = K.op("act", ACT(lsp[:], ef[:], AF.Ln, bias=1.0, scale=1.0), waits=[t_fexp[c]], tok=True)
            K.cut(11, [t])
            init = 0.0 if c == 0 else Ec[(c - 1) % 2][:, 511:512]
            t = K.op("dve", (lambda o, i: (lambda e: e.tensor_tensor_scan(
                out=o, data0=ones16[:], data1=lsp[:], initial=i, op0=ALU.mult, op1=ALU.add)))(Ec[c % 2][:], init),
                waits=[t], tok=True)
            t_scan[c] = t
            K.cut(12, [t])
            t = K.op("dve", TS(e8[:], Ec[c % 2][:], 8.0, ALU.mult), waits=[t, t_cstore.get(c - 1)], tok=True)
            t = K.op("dve", CP(TKt[:, 0, :], e8[:]), waits=[t], tok=True)
            t = K.op("dve", TT(r1[:], e8[:], TKt[:, 0, :], ALU.subtract), waits=[t], tok=True)
            t = K.op("dve", CP(TKt[:, 1, :], r1[:]), waits=[t], tok=True)
            t = K.op("dve", TT(r2[:], r1[:], TKt[:, 1, :], ALU.subtract), waits=[t], tok=True)
            t = K.op("dve", CP(TKt[:, 2, :], r2[:]), waits=[t], tok=True)
            t = K.op("dve", TS(TQt[:], TKt[:], -1.0, ALU.mult), waits=[t], tok=True)
            K.cut(13, [t])
            K.dma("sp", caug[:, 0, :, c * 512:(c + 1) * 512], TQt[:], s_cst, waits=[t])
            t_cstore[c] = K.dma("sp", caug[:, 1, :, c * 512:(c + 1) * 512], TKt[:], s_cst, waits=[t])
        t_A0 = [t_hst[NCH - 1], t_hst.get(NCH - 2), t_cstore[NCH - 1], t_fmm[NCH - 1]]
        if fz is not None:
            t_zero = K.dma("sp", agin[:, 0, :, 0:128].rearrange("p f t -> f p t"), zt[:], s_z, waits=[tz])

        if True:
            K.cut(0, t_A0)
        ipn = 0
        ev = {}
        hcn = 0
        t_pechunk = {}
        tcnt = 0
        gcnt = 0
        t_exp = {}
        t_pv = {}
        t_rel = {}
        t_y = None
        t_att_pe = None
        t_wload = {}
        t_ystore = {}

        def load_w(p):
            t = None
            for k4 in range(8):
                t = K.dma("pool", wbuf[p % 2][:, k4 * 2:(k4 + 1) * 2, :], wA[p, :, k4 * 2:(k4 + 1) * 2, :],
                          s_w[p % 2], waits=[t_wfree.get(p - 2)])
            t_wload[p] = t

        t_wfree = {}
        t_hload = {}

        def h_load(i):
            if i < NP * NCH:
                t_hload[i] = K.dma("sp", hbuf[i % 2][:], hT_d[i % NCH], s_h[i % 2], waits=[t_pechunk.get(i - 2)] + t_A0)

        load_w(0)
        for p in range(NP):
            hA, hB = 2 * p, 2 * p + 1
            if p + 1 < NP:
                load_w(p + 1)
            if fz is not None:
                fz["emit_P"](p)
            wa = [t_att_pe, t_set] + t_A0
            K.dma("sp", QA[64:67, :], caug[hA, 0], s_aug, waits=wa)
            K.dma("sp", KA[67:70, :], caug[hA, 1], s_aug)
            K.dma("sp", QB[0:3, :], caug[hB, 0], s_aug)
            t_aug = K.dma("sp", KBt[3:6, :], caug[hB, 1], s_aug)
            evw = [t_att_pe, t_y, t_set]
            K.cut(20, [t_aug, t_wload[p]])
            for c in range(NCH):
                hb_ = hbuf[hcn % 2]
                if hcn == 0:
                    h_load(0)
                    h_load(1)
                t_hl = t_hload[hcn]
                cols = slice(c * 512, (c + 1) * 512)
                for gi, wo in enumerate((0, 128, 384)):
                    acc = ip[:, ipn % 2, :]
                    tok = None
                    for kc in range(KC):
                        tok = K.op("pe", MM(acc, wbuf[p % 2][:, kc, wo:wo + 128], hb_[:, kc, :], kc == 0, kc == KC - 1),
                                   waits=[t_hl, t_wload[p]] + list(ev.get(ipn - 2, ())) if kc == 0 else (),
                                   tok=(kc == KC - 1))
                    if gi == 0:
                        a = K.op("act", ACT(QA[0:64, cols], acc[0:64, :], AF.Copy), waits=[tok] + evw, tok=True)
                        d = K.op("dve", CP(QB[64:128, cols], acc[64:128, :]), waits=[tok] + evw, tok=True)
                        ev[ipn] = (a, d)
                    elif gi == 1:
                        a = K.op("act", ACT(KA[0:64, cols], acc[0:64, :], AF.Copy), waits=[tok] + evw, tok=True)
                        d = K.op("dve", CP(KBt[64:128, cols], acc[64:128, :]), waits=[tok] + evw, tok=True)
                        ev[ipn] = (a, d)
                    else:
                        a = K.op("act", ACT(sg[:, cols], acc, AF.Silu), waits=[tok] + evw, tok=True)
                        ev[ipn] = (a,)
                    ipn += 1
                    K.cut(21 + gi, list(ev[ipn - 1]))
                acc = ip[:, ipn % 2, :].rearrange("p (b f) -> p b f", b=4)
                tok = None
                for bi in range(4):
                    for kc in range(KC):
                        last = (bi == 3 and kc == KC - 1)
                        tok = K.op("pe", MM(acc[:, bi, :], hb_[:, kc, bi * 128:(bi + 1) * 128],
                                            wbuf[p % 2][:, kc, 256:384], kc == 0, kc == KC - 1),
                                   waits=list(ev.get(ipn - 2, ())) if (bi == 0 and kc == 0) else (), tok=last)
                t_pechunk[hcn] = tok
                K.cut(24, [tok])
                a = K.op("act", ACT(VA[:, 4 * c:4 * c + 4, 0:64], acc[:, :, 0:64], AF.Copy), waits=[tok] + evw, tok=True)
                K.cut(25, [a])
                a = K.op("act", ACT(VB[:, 4 * c:4 * c + 4, 64:128], acc[:, :, 64:128], AF.Copy), tok=True)
                ev[ipn] = (a,)
                ipn += 1
                h_load(hcn + 2)
                hcn += 1
            t_wfree[p] = t_pechunk[hcn - 1]
            ev_done = list(ev[ipn - 1]) + list(ev[ipn - 2]) + list(ev[ipn - 3]) + list(ev[ipn - 4])

            K.cut(1, ev_done)
            tiles = []
            for X in (0, 1):
                for G in range(NCH):
                    for jj in range(2 * G + 2):
                        tiles.append((X, G, jj))

            def emit_qk(X, G, jj, n):
                Qt, Kt = (QA, KA) if X == 0 else (QB, KBt)
                rows = slice(0, 70) if X == 0 else slice(0, 128)
                sb_ = stv[n % NSB]
                tok = None
                first = True
                for sl in range(2):
                    j = 2 * jj + sl
                    r = j - 4 * G
                    w0 = [t_exp.get(n - NSB), t_aug] + ev_done if first else ()
                    first = False
                    kcols = slice(j * 128, (j + 1) * 128)
                    if r < 0:
                        tok = K.op("pe", MM(sb_[:, sl, :], Kt[rows, kcols], Qt[rows, G * 512:(G + 1) * 512], True, True),
                                   waits=w0, tok=(sl == 1))
                    else:
                        dsl = slice(r * 128, (r + 1) * 128)
                        K.op("pe", MM(sb_[:, sl, dsl], ident, cmask, True, False), waits=w0)
                        tok = K.op("pe", MM(sb_[:, sl, dsl], Kt[rows, kcols],
                                            Qt[rows, G * 512 + r * 128:G * 512 + (r + 1) * 128], False, True),
                                   tok=(sl == 1 and r == 3))
                        if r < 3:
                            tok = K.op("pe", MM(sb_[:, sl, (r + 1) * 128:512], Kt[rows, kcols],
                                                Qt[rows, G * 512 + (r + 1) * 128:(G + 1) * 512], True, True),
                                       tok=(sl == 1))
                return tok

            def emit_exp(X, G, jj, n, t_qk):
                buf = n % NSB
                sb_ = stv[buf]
                w = [t_qk, t_pv.get(n - NSB)]
                if 2 * jj + 1 < 4 * G:
                    return K.op("act", ACT(pt[buf][:], sb_, AF.Exp, scale=0.125), waits=w, tok=True)
                tok = None
                for sl in range(2):
                    r = 2 * jj + sl - 4 * G
                    tok = K.op("act", ACT(pt[buf][:, sl, r * 128:512], sb_[:, sl, r * 128:512], AF.Exp, scale=0.125),
                               waits=w if sl == 0 else (), tok=(sl == 1))
                return tok

            def emit_pv(X, G, jj, n, t_e, g):
                Vt = VA if X == 0 else VB
                buf = n % NSB
                tok = None
                for sl in range(2):
                    j = 2 * jj + sl
                    r = max(0, j - 4 * G)
                    w = [t_e, t_rel.get(g - 2)] if sl == 0 else ()
                    tok = K.op("pe", MM(ob[:, g % 2, r * 128:512], Vt[:, j, :], pt[buf][:, sl, r * 128:512],
                                        j == 0, j == 4 * G + 3), waits=w, tok=(sl == 1))
                return tok

            def emit_norm(X, G, g, t_last):
                nonlocal t_y
                o = ob[:, g % 2, :]
                cols = slice(G * 512, (G + 1) * 512)
                ro, rd = (slice(0, 64), slice(64, 128)) if X == 0 else (slice(64, 128), slice(0, 64))
                t = K.op("dve", RCP(rt[rd, :], o[rd, :]), waits=[t_last, t_y], tok=True)
                t = K.op("dve", TT(t1[ro, :], o[ro, :], rt[rd, :], ALU.mult), waits=[t], tok=True)
                t_rel[g] = t
                t_y = K.op("dve", TT(yp[p % 2][ro, cols], t1[ro, :], sg[ro, cols], ALU.mult),
                           waits=[t, t_ystore.get(p - 1)], tok=True)

            NTI = len(tiles)
            qk_tok = {}
            for i0 in range(min(NSB - 1, NTI)):
                qk_tok[i0] = emit_qk(*tiles[i0], tcnt + i0)
            for i in range(NTI):
                X, G, jj = tiles[i]
                n = tcnt + i
                if i + NSB - 1 < NTI:
                    qk_tok[i + NSB - 1] = emit_qk(*tiles[i + NSB - 1], n + NSB - 1)
                t_exp[n] = emit_exp(X, G, jj, n, qk_tok[i])
                g = gcnt + X * NCH + G
                t_pv[n] = emit_pv(X, G, jj, n, t_exp[n], g)
                if jj == 2 * G + 1:
                    emit_norm(X, G, g, t_pv[n])
            tcnt += NTI
            gcnt += 2 * NCH
            t_att_pe = t_pv[tcnt - 1]
            if fz is None:
                t_ystore[p] = K.dma("sp", y0T[p], yp[p % 2][:], s_y[p % 2], waits=[t_y])
            else:
                K.dma("sp", agin[p, 0, :, 128:SECL], yp[p % 2][:, 0:half], s_y[p % 2], waits=[t_y])
                t_ystore[p] = K.dma("sp", agin[p, 1, :, 0:SECL], yp[p % 2][:, half - 128:S], s_y[p % 2])
                K.raw("pool", (lambda pi: (lambda e: e.collective_compute(
                    "AllGather", ALU.bypass, replica_groups=fz["groups"],
                    ins=[agin[pi].rearrange("s f t -> (s f) t")],
                    outs=[agout[pi].rearrange("r s f t -> (r s f) t")])))(p),
                    fz["cc"], 1, waits=[t_ystore[p], t_zero])
        if fz is None:
            K._waits("sp", [t_ystore[NP - 1], t_ystore.get(NP - 2)])
            K.flush()
        else:
            fz["bar"] = [t_att_pe, t_exp[tcnt - 1], t_y, t_ystore[NP - 1], t_ystore.get(NP - 2)]


def make_cst():
    ident = np.eye(128, dtype=np.float32)
    s = np.arange(128)[:, None]
    t = np.arange(128)[None, :]
    cmask = np.where(s <= t, 0.0, NEG).astype(np.float32)
    return np.concatenate([ident, cmask], axis=1).astype(NPBF)


def kc_layout(w):
    n = w.shape[1]
    return np.ascontiguousarray(w.reshape(KC, 128, n).transpose(1, 0, 2))


def prep_A(x_b, norm_g0, w_in, b_f, hh, NP=8):
    W = 2048
    wl = []
    for p in range(NP):
        c0 = (hh * 16 + 2 * p) * 64
        cols = np.concatenate([w_in[:, c0:c0 + 128], w_in[:, W + c0:W + c0 + 128],
                               w_in[:, 2 * W + c0:2 * W + c0 + 128], w_in[:, 3 * W + c0:3 * W + c0 + 128]], axis=1)
        wl.append(kc_layout(cols))
    return {
        "x": np.ascontiguousarray(x_b),
        "g0": np.ascontiguousarray(norm_g0[None, :]),
        "wA": np.stack(wl),
        "wf": kc_layout(w_in[:, 4 * W + hh * 16:4 * W + hh * 16 + 16]),
        "bf": np.ascontiguousarray(b_f[hh * 16:hh * 16 + 16][:, None]),
        "cst": make_cst(),
    }


def build_B(NCHK=4, dbg=9):
    NTOK = 128 + 512 * NCHK
    nc = bass.Bass("TRN2", target_bir_lowering=False)
    yT_in = nc.dram_tensor("yT", [128, KC, NTOK], BF16, kind="ExternalInput").ap()
    xr = nc.dram_tensor("xr", [NTOK, D], F32, kind="ExternalInput").ap()
    wo0 = nc.dram_tensor("wo0", [4, 128, KC, 512], F32, kind="ExternalInput").ap()
    wi1 = nc.dram_tensor("wi1", [10, 128, KC, 512], F32, kind="ExternalInput").ap()
    wo1 = nc.dram_tensor("wo1", [4, 128, KC, 512], F32, kind="ExternalInput").ap()
    g1 = nc.dram_tensor("g1", [1, D], F32, kind="ExternalInput").ap()
    gf = nc.dram_tensor("gf", [1, D], F32, kind="ExternalInput").ap()
    snk = nc.dram_tensor("snk", [1, 32], F32, kind="ExternalInput").ap()
    cs = nc.dram_tensor("cs", [128, 2, NTOK], F32, kind="ExternalInput").ap()
    hbias = nc.dram_tensor("hbias", [128, 1], F32, kind="ExternalInput").ap()
    cst2 = nc.dram_tensor("cst2", [128, 1280], BF16, kind="ExternalInput").ap()
    out = nc.dram_tensor("out", [512 * NCHK, D], F32, kind="ExternalOutput").ap()
    wsc = nc.dram_tensor("wsc", [18, 128, KC, 512], BF16).ap()
    K = KB(nc)
    K.dbg = dbg
    try:
        with K.es:
            _build_B_body(K, nc, NCHK, NTOK, yT_in, xr, wo0, wi1, wo1, g1, gf, snk, cs, hbias, cst2, out, wsc)
    except _Cut:
        pass
    return nc


def _build_B_body(K, nc, NCHK, NTOK, yT_in, xr, wo0, wi1, wo1, g1, gf, snk, cs, hbias, cst2, out, wsc, fz=None):
    cst_sb = K.sb("cst_sb", [128, 1280], BF16)
    ident = cst_sb[:, 0:128]
    mrep = [cst_sb[:, 640:1152], cst_sb[:, 128:640]]
    Pm = cst_sb[:, 1152:1280]
    gbc1 = K.sb("gbc1", [128, D], F32)
    gbcf = K.sb("gbcf", [128, D], F32)
    es_bc = K.sb("es_bc", [128, 32], F32)
    hb_sb = K.sb("hb_sb", [128, 1], F32)
    wbuf = [K.sb("wbuf%d" % i, [128, KC, 512], BF16) for i in range(2)]
    yq = K.sb("yq", [128, KC, 512], BF16)
    hy = K.sb("hy", [128, KC, 512], BF16)
    sg1 = K.sb("sg1", [128, KC, 512], BF16)
    xs = K.sb("xs", [128, 4, D], F32)
    kTe = K.sb("kTe", [128, 4, 640], BF16)
    kTo = K.sb("kTo", [128, 4, 640], BF16)
    Vt = K.sb("Vt", [128, 5, 4, 128], BF16)
    cs_sb = K.sb("cs_sb", [128, 2, 512], F32)
    qb = [K.sb("qb%d" % i, [128, 512], BF16) for i in range(2)]
    t1r = [K.sb("t1r%d" % i, [128, 512], F32) for i in range(2)]
    t2r = [K.sb("t2r%d" % i, [128, 512], F32) for i in range(2)]
    pt1 = [K.sb("pt1_%d" % i, [128, 2, 4, 128], BF16) for i in range(2)]
    rt2 = [K.sb("rt%d" % i, [128, 4, 128], F32) for i in range(2)]
    t1 = K.sb("t1", [128, 4, 128], F32)
    dsb = [K.sb("dsb%d" % i, [128, 512], F32) for i in range(2)] if F_BATT else None
    pa = K.ps("pa", [128, 4, 512], F32)
    ob1 = K.ps("ob1", [128, 2, 512], F32)
    tpp = K.ps("tpp", [128, 2, 512], F32)
    tp1 = [tpp[:].rearrange("p a b -> p (a b)").bitcast(BF16).rearrange("p (k t) -> p k t", k=KC)]
    st1 = [pa[:, 2 * i:2 * i + 2, :].rearrange("p k (g t) -> p k g t", g=4) for i in range(2)]
    ob1v = [ob1[:, 0, :], ob1[:, 1, :], tpp[:, 0, :], tpp[:, 1, :]]
    NOB = 2
    s_c = K.sem("s_c")
    s_wc = [K.sem("s_wc%d" % i) for i in range(2)]
    s_ws = [K.sem("s_ws%d" % i) for i in range(2)]
    s_wl = [K.sem("s_wl%d" % i) for i in range(2)]
    s_in = K.sem("s_in")
    s_in2 = K.sem("s_in2")
    s_in3 = K.sem("s_in3")
    s_out = K.sem("s_out")

    K.dma("sp", cst_sb[:], cst2, s_c)
    K.dma("sp", gbc1[:], g1.partition_broadcast(128), s_c)
    K.dma("sp", gbcf[:], gf.partition_broadcast(128), s_c)
    K.dma("sp", es_bc[:], snk.partition_broadcast(128), s_c)
    if fz is not None:
        sel = K.sb("sel", [128, 2], F32)
        K.dma("sp", sel[:], fz["sel"], s_c)
    t_c = K.dma("sp", hb_sb[:], hbias, s_c)
    t_es = K.op("act", ACT(es_bc[:], es_bc[:], AF.Exp), waits=[t_c], tok=True)
    K.op("pool", MSET(kTe[:], 0.0))
    K.op("pool", MSET(kTo[:], 0.0))
    t_set = K.op("pool", MSET(Vt[:, :, :, 64:128], 1.0), tok=True)

    t_wst = {}
    if fz is None:
        for t in range(18):
            src = wo0[t] if t < 4 else (wi1[t - 4] if t < 14 else wo1[t - 14])
            tl = None
            for k in range(8):
                tl = K.dma("pool", wbuf[t % 2][:, 2 * k:2 * k + 2, :], src[:, 2 * k:2 * k + 2, :], s_wc[t % 2],
                           waits=[t_wst.get(t - 2)])
            t_wst[t] = K.dma("sp", wsc[t], wbuf[t % 2][:], s_ws[t % 2], waits=[tl])
        t_P = [t_wst[16], t_wst[17]]
    else:
        t_P = [fz["t_P"]]
    K.cut(0, t_P)

    NTn = NormT(K, gbc1, ident, tp1, "n1", nhb=1)
    NTn.junk = NTn.junk

    st = {"ipn": 0, "rn": 0, "ev": {}, "t_pool_rope": {}, "t_pm": {}, "t_d2": {}, "pending": None,
          "tile": 0, "t_exp": {}, "t_pv": {}, "t_norm": {}, "t_lastattn": None, "t_y": None,
          "t_xfree": None, "t_yqfree": None, "t_hyfree": None, "t_out": None}
    steps = []

    st["ypre"] = {}
    st["t_dc"] = {}
    st["t_ln"] = {}

    def y_prefetch(g):
        nt_ = 128 if g == 0 else 512
        t0_ = 0 if g == 0 else (1 + 4 * (g - 1)) * 128
        agout = fz["agout"]
        tl_ = None
        for sec, dstt in ((0, yq), (1, sg1)):
            for r in range(2):
                tl_ = K.dma("sp", dstt[:, r * 8:(r + 1) * 8, 0:nt_],
                            agout[:, r, sec, :, t0_:t0_ + nt_].rearrange("p f t -> f p t"), s_in,
                            waits=[st["t_yqfree"], st["t_yqfree_pe"], st["t_y"], fz["t_cc"]])
        st["ypre"][g] = {"t_dma": tl_, "nt": nt_}

    def y_blend(g):
        e_ = st["ypre"][g]
        nt_ = e_["nt"]
        b1 = K.op("dve", TS(yq[:, :, 0:nt_], yq[:, :, 0:nt_], sel[:, 0:1], ALU.mult), waits=[e_["t_dma"], t_c], tok=True)
        e_["t_y"] = K.op("dve", STT(yq[:, :, 0:nt_], sg1[:, :, 0:nt_], sel[:, 1:2], yq[:, :, 0:nt_],
                                    ALU.mult, ALU.add), waits=[b1], tok=True)

    def group_steps(gi):
        halo = (gi == 0)
        nb = 1 if halo else 4
        NT = nb * 128
        lb0 = 0 if halo else 1 + 4 * (gi - 1)
        tk0 = lb0 * 128
        G = {}

        def f_out0(n):
            def fn(wb, tl):
                if n == 0 and fz is not None:
                    if gi not in st["ypre"]:
                        y_prefetch(gi)
                    if "t_y" not in st["ypre"][gi]:
                        y_blend(gi)
                    G["t_y"] = st["ypre"][gi]["t_y"]
                if n == 0:
                    if fz is None:
                        G["t_y"] = K.dma("sp", yq[:, :, 0:NT], yT_in[:, :, tk0:tk0 + NT], s_in,
                                         waits=[st["t_yqfree"], st["t_yqfree_pe"]])
                    G["t_cs"] = K.dma("sp", cs_sb[:, :, 0:NT], cs[:, :, tk0:tk0 + NT], s_in2, waits=[st.get("t_lastrope")])
                    G["t_x"] = K.dma("sp", xs[:, 0:nb, :], xr[tk0:tk0 + NT, :].rearrange("(i p) d -> p i d", p=128),
                                     s_in3, waits=[st["t_out"], st["t_xfree"]])
                    G["t_add"] = {}
                tok = None
                for blk in range(nb):
                    acc = pa[:, st["ipn"] % 2, :]
                    for kc in range(KC):
                        tok = K.op("pe", MM(acc, yq[:, kc, blk * 128:(blk + 1) * 128], wb[:, kc, :], kc == 0, kc == KC - 1),
                                   waits=[tl, G["t_y"], st["t_lastattn"]] + list(st["ev"].get(st["ipn"] - 2, ())) if kc == 0 else (),
                                   tok=(kc == KC - 1))
                    xv = xs[:, blk, n * 512:(n + 1) * 512]
                    d = K.op("dve", TT(xv, acc, xv, ALU.add), waits=[tok, G["t_x"]], tok=True)
                    st["ev"][st["ipn"]] = (d,)
                    G["t_add"][blk] = d
                    st["ipn"] += 1
                if n == 3:
                    st["t_yqfree_pe"] = tok
                    pre_ = NTn.stats_dve(NTn.stats_act(xs[:, 0, :], [G["t_add"][0], t_c]))
                    for blk in range(nb):
                        sa_ = NTn.stats_act(xs[:, blk + 1, :], [G["t_add"][blk + 1], t_c]) if (blk + 1 < nb and F_BNORM) else None
                        G["t_h"] = NTn.emit(xs[:, blk, :], hy[:, :, blk * 128:(blk + 1) * 128],
                                            waits=[], dst_waits=[st["t_hyfree_pe"]], pre=pre_)
                        if sa_ is not None:
                            pre_ = NTn.stats_dve(sa_)
                        elif blk + 1 < nb:
                            pre_ = NTn.stats_dve(NTn.stats_act(xs[:, blk + 1, :], [G["t_add"][blk + 1], t_c]))
                    st["t_xfree"] = G["t_h"]
                return tok
            return fn

        def rope_first(acc, tok, dst, dst_waits):
            rn = st["rn"]
            a = K.op("act", ACT(qb[rn % 2][:, 0:NT], acc[:, 0:NT], AF.Copy), waits=[tok, st["t_pm"].get(rn - 2)], tok=True)
            d1 = K.op("dve", TT(t1r[rn % 2][:, 0:NT], acc[:, 0:NT], cs_sb[:, 0, 0:NT], ALU.mult),
                      waits=[tok, a, st["t_pool_rope"].get(rn - 2), G["t_cs"]], tok=True)
            st["pending"] = (rn, a, d1, dst, dst_waits)
            st["rn"] += 1
            return (a, d1)

        def rope_second():
            if st["pending"] is None:
                return
            rn, a, d1, dst, dst_waits = st["pending"]
            st["pending"] = None
            pp = pa[:, 2 + rn % 2, :]
            pm = K.op("pe", MM(pp[:, 0:NT], Pm, qb[rn % 2][:, 0:NT], True, True), waits=[a, st["t_d2"].get(rn - 2)], tok=True)
            st["t_pm"][rn] = pm
            d2 = K.op("dve", TT(t2r[rn % 2][:, 0:NT], pp[:, 0:NT], cs_sb[:, 1, 0:NT], ALU.mult),
                      waits=[pm, st["t_pool_rope"].get(rn - 2)], tok=True)
            st["t_d2"][rn] = d2
            pl = None
            for (r0_, r1_, dap) in dst:
                pl = K.op("dve", TT(dap, t1r[rn % 2][r0_:r1_, 0:NT], t2r[rn % 2][r0_:r1_, 0:NT], ALU.add),
                          waits=[d1, d2] + list(dst_waits), tok=True)
            st["t_pool_rope"][rn] = pl
            st["t_lastrope"] = pl

        def f_in(wt):
            def fn(wb, tl):
                tok = None
                if halo and wt == 4 and fz is not None and NCHK >= 1:
                    y_prefetch(1)
                if wt == 9:
                    for blk in range(nb):
                        acc = pa[:, st["ipn"] % 2, 0:256]
                        for kc in range(KC):
                            tok = K.op("pe", MM(acc, hy[:, kc, blk * 128:(blk + 1) * 128], wb[:, kc, 0:256], kc == 0, kc == KC - 1),
                                       waits=[tl, G["t_h"]] + list(st["ev"].get(st["ipn"] - 2, ())) if kc == 0 else (),
                                       tok=(kc == KC - 1))
                        slot = 0 if halo else 1 + blk
                        a = K.op("act", ACT(Vt[:, slot, :, 0:64], acc.rearrange("p (h d) -> p h d", h=4), AF.Copy),
                                 waits=[tok, t_set, st["t_lastattn"], st.get("t_ring")], tok=True)
                        st["ev"][st["ipn"]] = (a,)
                        st["ipn"] += 1
                    G["t_v"] = a
                    rope_second()
                    if halo and fz is not None and 1 in st["ypre"]:
                        y_blend(1)
                    return tok
                for fl in range(4):
                    acc = pa[:, st["ipn"] % 2, :]
                    for kc in range(KC):
                        tok = K.op("pe", MM(acc[:, 0:NT], wb[:, kc, fl * 128:(fl + 1) * 128], hy[:, kc, 0:NT], kc == 0, kc == KC - 1),
                                   waits=[tl, G["t_h"]] + list(st["ev"].get(st["ipn"] - 2, ())) if kc == 0 else (),
                                   tok=(kc == KC - 1))
                    rope_second()
                    if wt < 4:
                        f = 4 * wt + fl
                        evt = rope_first(acc, tok, [(0, 128, yq[:, f, 0:NT])], [st["t_yqfree_pe"]])
                    elif wt == 4:
                        k0 = 0 if halo else 128
                        evt = rope_first(acc, tok, [(0, 64, kTe[0:64, fl, k0:k0 + NT]), (64, 128, kTo[64:128, fl, k0:k0 + NT])],
                                         [st["t_lastattn"], st.get("t_ring"), t_set])
                    else:
                        f = 4 * (wt - 5) + fl
                        a = K.op("act", ACT(sg1[:, f, 0:NT], acc[:, 0:NT], AF.Silu), waits=[tok, st.get("t_y1"), G["t_y"]], tok=True)
                        evt = (a,)
                    st["ev"][st["ipn"]] = evt
                    st["ipn"] += 1
                return tok
            return fn

        def attention():
            rope_second()
            q_ready = [st["t_lastrope"], G["t_v"]]
            tiles = [(blk, h, hf) for blk in range(4) for h in range(4) for hf in range(2)]

            def e_qk(blk, h, hf, n):
                buf = n % 2
                tok = None
                for kbi in range(2):
                    kcol = blk * 128 + kbi * 128
                    K.op("pe", MM(st1[buf][:, kbi].rearrange("p g t -> p (g t)"), ident, mrep[kbi], True, False),
                         waits=[st["t_exp"].get(n - 2)] + q_ready + list(st["ev"].get(st["ipn"] - 1, ())) + list(st["ev"].get(st["ipn"] - 2, ()))
                         if kbi == 0 else ())
                    for gg in range(4):
                        hq = 8 * h + 4 * hf + gg
                        f = hq // 2
                        kz = kTe if hq % 2 == 0 else kTo
                        tok = K.op("pe", MM(st1[buf][:, kbi, gg, :], kz[:, h, kcol:kcol + 128],
                                            yq[:, f, blk * 128:(blk + 1) * 128], False, True),
                                   tok=(kbi == 1 and gg == 3))
                return tok

            def e_exp(blk, h, hf, n, tq):
                buf = n % 2
                w = [tq, st["t_pv"].get(n - 2)]
                if gi == 1 and blk == 0:
                    K.op("act", ACT(pt1[buf][:, 0], st1[buf][:, 0], AF.Exp, scale=0.125, bias=hb_sb[:, 0:1]), waits=w + [t_c])
                    return K.op("act", ACT(pt1[buf][:, 1], st1[buf][:, 1], AF.Exp, scale=0.125), tok=True)
                return K.op("act", ACT(pt1[buf][:], st1[buf], AF.Exp, scale=0.125), waits=w, tok=True)

            def e_pv(blk, h, hf, n, te):
                buf = n % 2
                tok = None
                for gg in range(4):
                    for kbi in range(2):
                        tok = K.op("pe", MM(ob1v[n % NOB][:, gg * 128:(gg + 1) * 128], Vt[:, blk + kbi, h, :], pt1[buf][:, kbi, gg, :],
                                            kbi == 0, kbi == 1),
                                   waits=[te, st["t_norm"].get(n - NOB)] if (gg == 0 and kbi == 0) else (),
                                   tok=(gg == 3 and kbi == 1))
                return tok

            def e_dcopy(n, tp_):
                st["t_dc"][n] = K.op("dve", CP(dsb[n % 2][64:128, :], ob1v[n % NOB][64:128, :]),
                                     waits=[tp_, st["t_ln"].get(n - 2)], tok=True)

            def e_norm(blk, h, hf, n, tp_):
                o = ob1v[n % NOB]
                rt = rt2[n % 2]
                cols = slice(blk * 128, (blk + 1) * 128)
                a = None
                dsrc = dsb[n % 2] if F_BATT else o
                if F_BATT:
                    tp_ = st["t_dc"][n]
                for gg in range(4):
                    hq = 8 * h + 4 * hf + gg
                    a = K.op("act", ACT(rt[64:128, gg, :], dsrc[64:128, gg * 128:(gg + 1) * 128], AF.Ln,
                                        bias=es_bc[64:128, hq:hq + 1], scale=1.0),
                             waits=[tp_, t_es, st["t_norm"].get(n - 2)] if gg == 0 else (), tok=(gg == 3))
                st["t_ln"][n] = a
                a = K.op("act", ACT(rt[64:128].rearrange("p g t -> p (g t)"), rt[64:128].rearrange("p g t -> p (g t)"),
                                    AF.Exp, scale=-1.0), waits=[a], tok=True)
                K.op("dve", TT(t1[0:64].rearrange("p g t -> p (g t)"), o[0:64, :],
                               rt[64:128].rearrange("p g t -> p (g t)"), ALU.mult), waits=[a, st["t_y"]])
                d = K.op("dve", TT(t1[64:128].rearrange("p g t -> p (g t)"), o[0:64, :],
                                   rt[64:128].rearrange("p g t -> p (g t)"), ALU.mult), tok=True)
                st["t_norm"][n] = d
                K.cut(203, [d])
                f0 = (8 * h + 4 * hf) // 2
                d = K.op("dve", TT(hy[0:64, f0:f0 + 2, cols], t1[0:64, 0:4:2, :], sg1[0:64, f0:f0 + 2, cols], ALU.mult),
                         waits=[d], tok=True)
                d = K.op("dve", TT(hy[64:128, f0:f0 + 2, cols], t1[64:128, 1:4:2, :], sg1[64:128, f0:f0 + 2, cols], ALU.mult),
                         tok=True)
                st["t_y"] = d

            n0 = st["tile"]
            tq = {0: e_qk(*tiles[0], n0)}
            for i, (blk, h, hf) in enumerate(tiles):
                n = n0 + i
                if i + 1 < len(tiles):
                    tq[i + 1] = e_qk(*tiles[i + 1], n + 1)
                st["t_exp"][n] = e_exp(blk, h, hf, n, tq[i])
                K.cut(200, [st["t_exp"][n]])
                st["t_pv"][n] = e_pv(blk, h, hf, n, st["t_exp"][n])
                K.cut(201, [st["t_pv"][n]])
                if not F_BATT:
                    e_norm(blk, h, hf, n, st["t_pv"][n])
                else:
                    e_dcopy(n, st["t_pv"][n])
                    if i > 0:
                        e_norm(*tiles[i - 1], n - 1, st["t_pv"][n - 1])
                    if i == len(tiles) - 1:
                        e_norm(blk, h, hf, n, st["t_pv"][n])
                K.cut(204, [st["t_y"]])
            st["tile"] += len(tiles)
            K.cut(205, [st["t_y"]])
            st["t_lastattn"] = st["t_pv"][st["tile"] - 1]
            st["t_yqfree"] = st["t_lastattn"]
            st["t_y1"] = st["t_y"]
            K.op("pool", CP(kTe[:, :, 0:128], kTe[:, :, 512:640]), waits=[st["t_lastattn"]])
            K.op("pool", CP(kTo[:, :, 0:128], kTo[:, :, 512:640]))
            r2_ = K.op("pool", CP(Vt[:, 0, :, 0:64], Vt[:, 4, :, 0:64]), tok=True)
            st["t_ring"] = r2_
            K.cut(206, [r2_])

        def f_out1(n):
            def fn(wb, tl):
                if n == 0:
                    attention()
                    G["t_add2"] = {}
                    if fz is not None and gi + 1 <= NCHK:
                        y_prefetch(gi + 1)
                tok = None
                for blk in range(4):
                    acc = pa[:, st["ipn"] % 2, :]
                    for kc in range(KC):
                        tok = K.op("pe", MM(acc, hy[:, kc, blk * 128:(blk + 1) * 128], wb[:, kc, :], kc == 0, kc == KC - 1),
                                   waits=[tl, st["t_y"], st["t_exp"][st["tile"] - 1], st["t_exp"][st["tile"] - 2]]
                                   + list(st["ev"].get(st["ipn"] - 2, ())) if kc == 0 else (),
                                   tok=(kc == KC - 1))
                    xv = xs[:, blk, n * 512:(n + 1) * 512]
                    d = K.op("dve", TT(xv, acc, xv, ALU.add), waits=[tok], tok=True)
                    st["ev"][st["ipn"]] = (d,)
                    G["t_add2"][blk] = d
                    st["ipn"] += 1
                if n == 2 and fz is not None and (gi + 1) in st["ypre"]:
                    y_blend(gi + 1)
                if n == 3:
                    st["t_hyfree_pe"] = tok
                    t_o = None
                    sas = [NTn.stats_act(xs[:, blk, :], [G["t_add2"][blk]]) for blk in range(4)]
                    for blk in range(4):
                        t, col = NTn.stats_dve(sas[blk])
                        d = K.op("dve", STT(xs[:, blk, :], xs[:, blk, :], NTn.rstd[:, col:col + 1], gbcf[:], ALU.mult, ALU.mult),
                                 waits=[t], tok=True)
                        r0 = (lb0 - 1 + blk) * 128
                        t_o = K.dma("sp", out[r0:r0 + 128, :], xs[:, blk, :], s_out, waits=[d])
                    st["t_out"] = t_o
                return tok
            return fn

        for n in range(4):
            steps.append((n, f_out0(n)))
        for wt in ((4, 9) if halo else range(10)):
            steps.append((4 + wt, f_in(wt)))
        if not halo:
            for n in range(4):
                steps.append((14 + n, f_out1(n)))

    st["t_hyfree_pe"] = None
    st["t_yqfree_pe"] = None
    for gi in range(1 + NCHK):
        group_steps(gi)

    t_load = {}
    t_use = {}

    def issue(i):
        t_load[i] = K.dma("sp", wbuf[i % 2][:], wsc[steps[i][0]], s_wl[i % 2], waits=t_P + [t_use.get(i - 2)])

    issue(0)
    for i, (widx, fn) in enumerate(steps):
        if i + 1 < len(steps):
            issue(i + 1)
        t_use[i] = fn(wbuf[i % 2], t_load[i])
        K.cut(100 + i, [t_use[i]])
    K._waits("sp", [st["t_out"]])
    K.flush()


def build_fused(S=4096, NP=8, groups=None):
    NB = S // 128
    NCH = S // 512
    half = S // 2
    NCHK = half // 512
    NTOK = half + 128
    SECL = NTOK
    if groups is None:
        groups = [[0, 1], [2, 3], [4, 5], [6, 7]]
    nc = bass.Bass("TRN2", target_bir_lowering=False)
    di = lambda name, shape, dt: nc.dram_tensor(name, shape, dt, kind="ExternalInput").ap()
    x = di("x", [S, D], F32)
    g0 = di("g0", [1, D], F32)
    wA = di("wA", [NP, 128, KC, 512], F32)
    wf = di("wf", [128, KC, 16], F32)
    bfv = di("bf", [16, 1], F32)
    cst = di("cst", [128, 256], BF16)
    xr = di("xr", [NTOK, D], F32)
    wo0 = di("wo0", [4, 128, KC, 512], F32)
    wi1 = di("wi1", [10, 128, KC, 512], F32)
    wo1 = di("wo1", [4, 128, KC, 512], F32)
    g1 = di("g1", [1, D], F32)
    gf = di("gf", [1, D], F32)
    snk = di("snk", [1, 32], F32)
    cs = di("cs", [128, 2, NTOK], F32)
    hbias = di("hbias", [128, 1], F32)
    cst2 = di("cst2", [128, 1280], BF16)
    sel_d = di("sel", [128, 2], F32)
    out = nc.dram_tensor("out", [half, D], F32, kind="ExternalOutput").ap()
    hT_d = nc.dram_tensor("hT_d", [NCH, 128, KC, 512], BF16).ap()
    caug = nc.dram_tensor("caug", [16, 2, 3, S], BF16).ap()
    wsc = nc.dram_tensor("wsc", [18, 128, KC, 512], BF16).ap()
    agin = nc.dram_tensor("agin", [NP, 2, 128, SECL], BF16).ap()
    agout = nc.dram_tensor("agout", [NP, 2, 2, 128, SECL], BF16).ap()

    K = KB(nc)
    with K.es:
        s_wp = K.sem("s_wp")
        cc = K.sem("cc")
        fz = {"agin": agin, "agout": agout, "groups": groups, "cc": cc, "sel": sel_d}
        per = [(18 * p) // NP for p in range(NP + 1)]

        def emit_P(p):
            for t in range(per[p], per[p + 1]):
                src = wo0[t] if t < 4 else (wi1[t - 4] if t < 14 else wo1[t - 14])
                for k in range(8):
                    fz["t_P"] = K.dma("pool", wsc[t][:, 2 * k:2 * k + 2, :], src[:, 2 * k:2 * k + 2, :], s_wp)

        fz["emit_P"] = emit_P
        with ExitStack() as esA:
            K.cur = esA
            K.prefix = "A_"
            _build_A_body(K, nc, S, NP, NB, NCH, x, g0, wA, wf, bfv, cst, None, hT_d, caug, 9, fz=fz)
            K.flush()
        fz["t_cc"] = (cc, cc.n)
        K.barrier(fz["bar"])
        with ExitStack() as esB:
            K.cur = esB
            K.prefix = "B_"
            _build_B_body(K, nc, NCHK, NTOK, None, xr, wo0, wi1, wo1, g1, gf, snk, cs, hbias, cst2, out, wsc, fz=fz)
        K.cur = None
    return nc


def make_cst2():
    ident = np.eye(128, dtype=np.float32)
    s = np.arange(128)[:, None]
    t = np.arange(128)[None, :]
    mcur = np.where(s <= t, 0.0, NEG).astype(np.float32)
    mprev = np.where(s > t, 0.0, NEG).astype(np.float32)
    Pm = np.zeros((128, 128), np.float32)
    for do in range(128):
        d = do % 64
        if d < 16:
            di = do + 8 if d < 8 else do - 8
            Pm[di, do] = 1.0
    return np.concatenate([ident, np.tile(mcur, (1, 4)), np.tile(mprev, (1, 4)), Pm], axis=1).astype(NPBF)


def rope_tables(pos):
    half = 8
    inv_freq = (np.float32(500000.0) ** (-np.arange(half, dtype=np.float32) / np.float32(half))).astype(np.float32)
    ang = (pos.astype(np.float32)[:, None] * inv_freq[None, :]).astype(np.float32)
    cos = np.cos(ang).astype(np.float32).T
    sin = np.sin(ang).astype(np.float32).T
    n = pos.shape[0]
    C = np.ones((128, n), np.float32)
    Sg = np.zeros((128, n), np.float32)
    for base in (0, 64):
        C[base:base + 8] = cos
        C[base + 8:base + 16] = cos
        Sg[base:base + 8] = -sin
        Sg[base + 8:base + 16] = sin
    return np.ascontiguousarray(np.stack([C, Sg], axis=1))


def tile_layout(w):
    n = w.shape[1] // 512
    return np.ascontiguousarray(w.reshape(KC, 128, n, 512).transpose(2, 1, 0, 3))


def prep_B_weights(w_out0, w_in1, w_out1):
    WQ, WK = 2048, 256
    q = w_in1[:, :WQ]
    k = w_in1[:, WQ:WQ + WK]
    v = w_in1[:, WQ + WK:WQ + 2 * WK]
    g = w_in1[:, WQ + 2 * WK:]
    kdup = np.concatenate([np.concatenate([k[:, h * 64:(h + 1) * 64]] * 2, axis=1) for h in range(4)], axis=1)
    vpad = np.concatenate([v, np.zeros((2048, 256), np.float32)], axis=1)
    wi = np.concatenate([q, kdup, g, vpad], axis=1)
    return {"wo0": tile_layout(w_out0), "wi1": tile_layout(wi), "wo1": tile_layout(w_out1)}


def prep_B(yT_full, x_rows, pos, hh, wts, g1, gf, sinks):
    ntok = x_rows.shape[0]
    d = dict(wts)
    d["yT"] = np.ascontiguousarray(yT_full.reshape(KC, 128, ntok).transpose(1, 0, 2))
    d["xr"] = np.ascontiguousarray(x_rows)
    d["g1"] = np.ascontiguousarray(g1[None, :])
    d["gf"] = np.ascontiguousarray(gf[None, :])
    d["snk"] = np.ascontiguousarray(sinks[None, :])
    d["cs"] = rope_tables(pos)
    d["hbias"] = np.full((128, 1), NEG if hh == 0 else 0.0, np.float32)
    d["cst2"] = make_cst2()
    return d


def kernel(x, norm_g, fox_w_in, fox_b_f, fox_w_out, swa_w_in, swa_sinks, swa_w_out, final_g):
    x = np.asarray(x, dtype=np.float32)
    norm_g = np.asarray(norm_g, dtype=np.float32)
    fox_w_in = np.asarray(fox_w_in, dtype=np.float32)
    fox_b_f = np.asarray(fox_b_f, dtype=np.float32)
    fox_w_out = np.asarray(fox_w_out, dtype=np.float32)
    swa_w_in = np.asarray(swa_w_in, dtype=np.float32)
    swa_sinks = np.asarray(swa_sinks, dtype=np.float32)
    swa_w_out = np.asarray(swa_w_out, dtype=np.float32)
    final_g = np.asarray(final_g, dtype=np.float32)
    B, S, _ = x.shape
    ncores = 8
    half = S // 2
    nc = build_fused(S, 8)
    wA_parts = [prep_A(x[0], norm_g[0], fox_w_in[0], fox_b_f[0], hh) for hh in range(2)]
    for wp in wA_parts:
        wp.pop("x")
    wtsB = prep_B_weights(fox_w_out[0], swa_w_in[0], swa_w_out[0])
    maps = []
    for c in range(ncores):
        b, hh = c // 2, c % 2
        maps.append(prep_fused(x[b], hh, S, wA_parts, wtsB, norm_g, final_g, swa_sinks[0]))
    res = run_bass_kernel_spmd(nc, maps, core_ids=list(range(ncores)))
    out = np.zeros((B, S, D), np.float32)
    for c in range(ncores):
        b, hh = c // 2, c % 2
        out[b, hh * half:(hh + 1) * half] = np.asarray(res.results[c]["out"]).reshape(half, D)
    return out


def prep_fused(x_b, hh, S, wA_parts, wtsB, norm_g, final_g, sinks):
    half = S // 2
    ntok = half + 128
    t0 = hh * half - 128
    toks = np.arange(t0, t0 + ntok)
    valid = toks >= 0
    xr = np.zeros((ntok, D), np.float32)
    xr[valid] = x_b[toks[valid]]
    m = dict(wA_parts[hh])
    m["x"] = np.ascontiguousarray(x_b)
    m.update(wtsB)
    m["xr"] = xr
    m["g1"] = np.ascontiguousarray(norm_g[1][None, :])
    m["gf"] = np.ascontiguousarray(final_g[None, :])
    m["snk"] = np.ascontiguousarray(sinks[None, :])
    m["cs"] = rope_tables(toks.astype(np.float32))
    m["hbias"] = np.full((128, 1), NEG if hh == 0 else 0.0, np.float32)
    m["cst2"] = make_cst2()
    sel = np.zeros((128, 2), np.float32)
    sel[:, hh] = 1.0
    m["sel"] = sel
    return m
```

```python
import numpy as np
import ml_dtypes
from contextlib import ExitStack
import concourse.bass as bass
import concourse.mybir as mybir
from concourse.bass_utils import run_bass_kernel_spmd

F32 = mybir.dt.float32
BF16 = mybir.dt.bfloat16
AF = mybir.ActivationFunctionType
ALU = mybir.AluOpType
NPBF = ml_dtypes.bfloat16

F_BATT = True
F_BNORM = True
D = 2048
KC = 16
EPS = 1e-6
NEG = -30000.0


class Sem:
    def __init__(self, h, name):
        self.h = h
        self.name = name
        self.n = 0


class _Cut(Exception):
    pass


class KB:
    ENG = ("pe", "act", "dve", "pool", "sp")
    dbg = None

    def cut(self, level, waits):
        if self.dbg == level:
            self._waits("sp", waits)
            self.flush()
            raise _Cut()

    def __init__(self, nc):
        self.nc = nc
        self.es = ExitStack()
        self.q = {e: [] for e in self.ENG}
        self.waited = {e: {} for e in self.ENG}
        self.esem = {}
        for e in ("pe", "act", "dve", "pool"):
            self.esem[e] = self.sem("e_" + e)

    prefix = ""
    cur = None

    def sem(self, name):
        name = self.prefix + name
        return Sem(self.es.enter_context(self.nc.semaphore(name)), name)

    def sb(self, name, shape, dt):
        return (self.cur or self.es).enter_context(self.nc.sbuf_tensor(self.prefix + name, shape, dt))

    def ps(self, name, shape, dt):
        return (self.cur or self.es).enter_context(self.nc.psum_tensor(self.prefix + name, shape, dt))

    def raw(self, eng, fn, sem, n, waits=()):
        self._waits(eng, waits)
        sem.n += n
        self.q[eng].append((1, fn, sem.h, n))
        return (sem, sem.n)

    def barrier(self, toks):
        for eng in self.ENG:
            self._waits(eng, toks)

    def _waits(self, eng, waits):
        for w in waits:
            if w is None:
                continue
            s, v = w
            if v <= 0 or self.waited[eng].get(s.name, 0) >= v:
                continue
            self.waited[eng][s.name] = v
            self.q[eng].append((0, s.h, v))

    def op(self, eng, fn, waits=(), tok=False):
        self._waits(eng, waits)
        if tok:
            s = self.esem[eng]
            s.n += 1
            self.q[eng].append((1, fn, s.h, 1))
            return (s, s.n)
        self.q[eng].append((1, fn, None, 0))
        return None

    def dma(self, eng, out, in_, sem, waits=()):
        self._waits(eng, waits)
        sem.n += 16
        self.q[eng].append((1, (lambda e: e.dma_start(out=out, in_=in_)), sem.h, 16))
        return (sem, sem.n)

    def flush(self):
        with self.nc.Block() as block:
            dec = {"pe": block.tensor, "act": block.scalar, "dve": block.vector,
                   "pool": block.gpsimd, "sp": block.sync}
            for eng in self.ENG:
                items = self.q[eng]
                self.q[eng] = []
                if not items:
                    continue

                def body(e, items=items):
                    for it in items:
                        if it[0] == 0:
                            e.wait_ge(it[1], it[2])
                        else:
                            ins = it[1](e)
                            if it[2] is not None:
                                ins.then_inc(it[2], it[3])

                dec[eng](body)


def MM(out, lhsT, rhs, start, stop):
    return lambda e: e.matmul(out, lhsT=lhsT, rhs=rhs, start=start, stop=stop, skip_group_check=True)


def TR(out, in_, ident):
    return lambda e: e.transpose(out, in_, ident)


def ACT(out, in_, func, **kw):
    return lambda e: e.activation(out=out, in_=in_, func=func, **kw)


def TT(out, in0, in1, op):
    return lambda e: e.tensor_tensor(out=out, in0=in0, in1=in1, op=op)


def TS(out, in0, s1, op0, s2=None, op1=None):
    if op1 is None:
        return lambda e: e.tensor_scalar(out=out, in0=in0, scalar1=s1, scalar2=None, op0=op0)
    return lambda e: e.tensor_scalar(out=out, in0=in0, scalar1=s1, scalar2=s2, op0=op0, op1=op1)


def STT(out, in0, scalar, in1, op0, op1):
    return lambda e: e.scalar_tensor_tensor(out=out, in0=in0, scalar=scalar, in1=in1, op0=op0, op1=op1)


def CP(out, in_):
    return lambda e: e.tensor_copy(out=out, in_=in_)


def RCP(out, in_):
    return lambda e: e.reciprocal(out=out, in_=in_)


def MSET(ap, c):
    return lambda e: e.memset(ap, c)


class NormT:
    def __init__(self, K, gbc, ident, tp2, name, nhb=2):
        self.K = K
        self.gbc = gbc
        self.ident = ident
        self.tp2 = tp2
        self.junk = K.sb(name + "_junk", [128, D], BF16)
        self.hb = [K.sb(name + "_hb%d" % i, [128, D], BF16) for i in range(nhb)]
        self.ss = K.sb(name + "_ss", [128, 64], F32)
        self.std = K.sb(name + "_std", [128, 64], F32)
        self.rstd = K.sb(name + "_rstd", [128, 64], F32)
        self.n = 0
        self.sc = 0
        self.t_tr = {}
        self.t_cp = {}
        self.t_stt = {}

    def stats_act(self, xs, waits):
        K = self.K
        col = self.sc % 64
        self.sc += 1
        t = K.op("act", ACT(self.junk[:], xs, AF.Square, accum_out=self.ss[:, col:col + 1]),
                 waits=waits, tok=True)
        t = K.op("act", ACT(self.std[:, col:col + 1], self.ss[:, col:col + 1], AF.Sqrt,
                            scale=1.0 / D, bias=EPS), waits=[t], tok=True)
        return t, col

    def stats_dve(self, tc):
        t, col = tc
        t = self.K.op("dve", RCP(self.rstd[:, col:col + 1], self.std[:, col:col + 1]), waits=[t], tok=True)
        return t, col

    def stats(self, xs, waits):
        return self.stats_dve(self.stats_act(xs, waits))

    def emit(self, xs, dst, waits, dst_waits=(), pre=None):
        K = self.K
        n = self.n
        t, col = pre if pre is not None else self.stats(xs, waits)
        K.cut(-4, [t])
        nh = len(self.hb)
        ntp = len(self.tp2)
        hb = self.hb[n % nh]
        tp = self.tp2[n % ntp]
        t_stt = K.op("dve", STT(hb[:], xs, self.rstd[:, col:col + 1], self.gbc[:],
                                ALU.mult, ALU.mult),
                     waits=[t, self.t_tr.get(n - nh)], tok=True)
        self.t_stt[n] = t_stt
        K.cut(-3, [t_stt])
        tok = None
        for kc in range(KC):
            tok = K.op("pe", TR(tp[:, kc, :], hb[:, kc * 128:(kc + 1) * 128], self.ident),
                       waits=[t_stt, self.t_cp.get(n - ntp)] if kc == 0 else (), tok=(kc == KC - 1))
        self.t_tr[n] = tok
        t_cp = K.op("act", ACT(dst, tp[:], AF.Copy), waits=[tok] + list(dst_waits), tok=True)
        self.t_cp[n] = t_cp
        K.cut(-2, [t_cp])
        self.n += 1
        return t_cp


def build_A(S=4096, NP=8, dbg=9):
    NB = S // 128
    NCH = S // 512
    NH = 2 * NP
    nc = bass.Bass("TRN2", target_bir_lowering=False)
    x = nc.dram_tensor("x", [S, D], F32, kind="ExternalInput").ap()
    g0 = nc.dram_tensor("g0", [1, D], F32, kind="ExternalInput").ap()
    wA = nc.dram_tensor("wA", [NP, 128, KC, 512], F32, kind="ExternalInput").ap()
    wf = nc.dram_tensor("wf", [128, KC, 16], F32, kind="ExternalInput").ap()
    bfv = nc.dram_tensor("bf", [16, 1], F32, kind="ExternalInput").ap()
    cst = nc.dram_tensor("cst", [128, 256], BF16, kind="ExternalInput").ap()
    y0T = nc.dram_tensor("y0T", [NP, 128, S], BF16, kind="ExternalOutput").ap()
    hT_d = nc.dram_tensor("hT_d", [NCH, 128, KC, 512], BF16).ap()
    caug = nc.dram_tensor("caug", [16, 2, 3, S], BF16).ap()

    K = KB(nc)
    K.dbg = dbg
    try:
      with K.es:
        _build_A_body(K, nc, S, NP, NB, NCH, x, g0, wA, wf, bfv, cst, y0T, hT_d, caug, dbg)
    except _Cut:
        pass
    return nc


def _build_A_body(K, nc, S, NP, NB, NCH, x, g0, wA, wf, bfv, cst, y0T, hT_d, caug, dbg, fz=None):
    if True:
        cst_sb = K.sb("cst_sb", [128, 256], BF16)
        ident = cst_sb[:, 0:128]
        cmask = cst_sb[:, 128:256]
        gbc = K.sb("gbc", [128, D], F32)
        NXB = 3
        xbuf = [K.sb("xbuf%d" % i, [128, D], F32) for i in range(NXB)]
        hbuf = [K.sb("hbuf%d" % i, [128, KC, 512], BF16) for i in range(2)]
        wfb = K.sb("wfb", [128, KC, 16], BF16)
        nb = K.sb("nb", [16, 2], F32)
        ef = K.sb("ef", [16, 512], F32)
        lsp = K.sb("lsp", [16, 512], F32)
        ones16 = K.sb("ones16", [16, 512], F32)
        Ec = [K.sb("Ec%d" % i, [16, 512], F32) for i in range(2)]
        e8 = K.sb("e8", [16, 512], F32)
        r1 = K.sb("r1", [16, 512], F32)
        r2 = K.sb("r2", [16, 512], F32)
        TKt = K.sb("TKt", [16, 3, 512], BF16)
        TQt = K.sb("TQt", [16, 3, 512], BF16)
        wbuf = [K.sb("wbuf%d" % i, [128, KC, 512], BF16) for i in range(2)]
        QA = K.sb("QA", [128, S], BF16)
        QB = K.sb("QB", [128, S], BF16)
        KA = K.sb("KA", [128, S], BF16)
        KBt = K.sb("KBt", [128, S], BF16)
        VA = K.sb("VA", [128, NB, 128], BF16)
        VB = K.sb("VB", [128, NB, 128], BF16)
        sg = K.sb("sg", [128, S], BF16)
        NSB = 3
        pt = [K.sb("pt%d" % i, [128, 2, 512], BF16) for i in range(NSB)]
        rt = K.sb("rt", [128, 512], F32)
        t1 = K.sb("t1", [128, 512], F32)
        yp = [K.sb("yp%d" % i, [128, S], BF16) for i in range(1)] * 2
        st = K.ps("st", [128, 2, 2, 512], F32)
        ob = K.ps("ob", [128, 2, 512], F32)
        ip = K.ps("ip", [128, 2, 512], F32)
        tp2 = [st[:, i].rearrange("p a b -> p (a b)").bitcast(BF16).rearrange("p (k t) -> p k t", k=KC)
               for i in range(2)]
        fps = ob[0:16, 0, :]
        stv = [st[:, 0], st[:, 1], ip[:]]
        s_c = K.sem("s_c")
        s_c2 = K.sem("s_c2")
        s_x = [K.sem("s_x%d" % i) for i in range(3)]
        s_hst = [K.sem("s_hst%d" % i) for i in range(2)]
        s_cst = K.sem("s_cst")
        s_w = [K.sem("s_w%d" % i) for i in range(2)]
        s_h = [K.sem("s_h%d" % i) for i in range(2)]
        s_aug = K.sem("s_aug")
        s_y = [K.sem("s_y%d" % i) for i in range(2)]

        t_c0 = K.dma("sp", cst_sb[:], cst, s_c)
        K.dma("sp", gbc[:], g0.partition_broadcast(128), s_c)
        t_c = K.dma("sp", nb[:, 0:1], bfv, s_c)
        t_c2 = K.dma("pool", wfb[:], wf, s_c2)
        t_nb = K.op("dve", TS(nb[:, 1:2], nb[:, 0:1], -1.0, ALU.mult), waits=[t_c], tok=True)
        K.op("dve", MSET(ones16[:], 1.0))
        K.op("pool", MSET(QA[64:70, :], 1.0))
        K.op("pool", MSET(KA[64:70, :], 1.0))
        K.op("pool", MSET(QB[0:64, :], 0.0))
        K.op("pool", MSET(KBt[0:64, :], 0.0))
        K.op("pool", MSET(VA[:, :, 64:128], 1.0))
        K.op("pool", MSET(VB[:, :, 0:64], 1.0), tok=True)
        K.op("pool", MSET(QB[0:6, :], 1.0), waits=[(K.esem["pool"], K.esem["pool"].n)])
        t_set = K.op("pool", MSET(KBt[0:6, :], 1.0), tok=True)
        if fz is not None:
            half = S // 2
            SECL = half + 128
            agin, agout = fz["agin"], fz["agout"]
            zt = K.sb("zt", [128, NP, 128], BF16)
            s_z = K.sem("s_z")
            tz = K.op("dve", MSET(zt[:], 0.0), tok=True)
        K.cut(-5, [t_set, t_nb, t_c0, t_c2])

        NT = NormT(K, gbc, ident[:, :] if False else ident, tp2, "n0")
        t_hst = {}
        t_fexp = {}
        t_cstore = {}
        t_scan = {}
        t_fmm = {}
        pre = {}
        t_xld = {}

        def x_load(b):
            if b < NB:
                t_xld[b] = K.dma("sp", xbuf[b % NXB][:], x[b * 128:(b + 1) * 128, :], s_x[b % NXB],
                                 waits=[NT.t_stt.get(b - NXB)])

        def x_stats_act(b):
            return NT.stats_act(xbuf[b % NXB][:], [t_xld[b], t_c]) if b < NB else None

        x_load(0)
        x_load(1)
        pre[0] = NT.stats_dve(x_stats_act(0))
        for b in range(NB):
            c, bi = b // 4, b % 4
            xs = xbuf[b % NXB]
            x_load(b + 2)
            sa = x_stats_act(b + 1)
            t_cp = NT.emit(xs[:], hbuf[c % 2][:, :, bi * 128:(bi + 1) * 128], waits=[],
                           dst_waits=[t_hst.get(c - 2), t_fmm.get(c - 2)] if bi == 0 else (), pre=pre[b])
            if sa is not None:
                pre[b + 1] = NT.stats_dve(sa)
            if bi != 3:
                continue
            t_hst[c] = K.dma("sp", hT_d[c], hbuf[c % 2][:], s_hst[c % 2], waits=[t_cp])
            tok = None
            for kc in range(KC):
                tok = K.op("pe", MM(fps, wfb[:, kc, :], hbuf[c % 2][:, kc, :], kc == 0, kc == KC - 1),
                           waits=[t_cp, t_c2, t_fexp.get(c - 1)] if kc == 0 else (), tok=(kc == KC - 1))
            t_fmm[c] = tok
            K.cut(-1, [tok, t_hst[c]])
            t_fexp[c] = K.op("act", ACT(ef[:], fps, AF.Exp, scale=-1.0, bias=nb[:, 1:2]),
                             waits=[tok, t_nb, t_scan.get(c - 1)], tok=True)
            K.cut(10, [t_fexp[c]])
            t = K.op("act", ACT(lsp[:], ef[:], AF.Ln, bias=1.0, scale=1.0), waits=[t_fexp[c]], tok=True)
            K.cut(11, [t])
            init = 0.0 if c == 0 else Ec[(c - 1) % 2][:, 511:512]
            t = K.op("dve", (lambda o, i: (lambda e: e.tensor_tensor_scan(
                out=o, data0=ones16[:], data1=lsp[:], initial=i, op0=ALU.mult, op1=ALU.add)))(Ec[c % 2][:], init),
                waits=[t], tok=True)
            t_scan[c] = t
            K.cut(12, [t])
            t = K.op("dve", TS(e8[:], Ec[c % 2][:], 8.0, ALU.mult), waits=[t, t_cstore.get(c - 1)], tok=True)
            t = K.op("dve", CP(TKt[:, 0, :], e8[:]), waits=[t], tok=True)
            t = K.op("dve", TT(r1[:], e8[:], TKt[:, 0, :], ALU.subtract), waits=[t], tok=True)
            t = K.op("dve", CP(TKt[:, 1, :], r1[:]), waits=[t], tok=True)
            t = K.op("dve", TT(r2[:], r1[:], TKt[:, 1, :], ALU.subtract), waits=[t], tok=True)
            t = K.op("dve", CP(TKt[:, 2, :], r2[:]), waits=[t], tok=True)
            t = K.op("dve", TS(TQt[:], TKt[:], -1.0, ALU.mult), waits=[t], tok=True)
            K.cut(13, [t])
            K.dma("sp", caug[:, 0, :, c * 512:(c + 1) * 512], TQt[:], s_cst, waits=[t])
            t_cstore[c] = K.dma("sp", caug[:, 1, :, c * 512:(c + 1) * 512], TKt[:], s_cst, waits=[t])
        t_A0 = [t_hst[NCH - 1], t_hst.get(NCH - 2), t_cstore[NCH - 1], t_fmm[NCH - 1]]
        if fz is not None:
            t_zero = K.dma("sp", agin[:, 0, :, 0:128].rearrange("p f t -> f p t"), zt[:], s_z, waits=[tz])

        if True:
            K.cut(0, t_A0)
        ipn = 0
        ev = {}
        hcn = 0
        t_pechunk = {}
        tcnt = 0
        gcnt = 0
        t_exp = {}
        t_pv = {}
        t_rel = {}
        t_y = None
        t_att_pe = None
        t_wload = {}
        t_ystore = {}

        def load_w(p):
            t = None
            for k4 in range(8):
                t = K.dma("pool", wbuf[p % 2][:, k4 * 2:(k4 + 1) * 2, :], wA[p, :, k4 * 2:(k4 + 1) * 2, :],
                          s_w[p % 2], waits=[t_wfree.get(p - 2)])
            t_wload[p] = t

        t_wfree = {}
        t_hload = {}

        def h_load(i):
            if i < NP * NCH:
                t_hload[i] = K.dma("sp", hbuf[i % 2][:], hT_d[i % NCH], s_h[i % 2], waits=[t_pechunk.get(i - 2)] + t_A0)

        load_w(0)
        for p in range(NP):
            hA, hB = 2 * p, 2 * p + 1
            if p + 1 < NP:
                load_w(p + 1)
            if fz is not None:
                fz["emit_P"](p)
            wa = [t_att_pe, t_set] + t_A0
            K.dma("sp", QA[64:67, :], caug[hA, 0], s_aug, waits=wa)
            K.dma("sp", KA[67:70, :], caug[hA, 1], s_aug)
            K.dma("sp", QB[0:3, :], caug[hB, 0], s_aug)
            t_aug = K.dma("sp", KBt[3:6, :], caug[hB, 1], s_aug)
            evw = [t_att_pe, t_y, t_set]
            K.cut(20, [t_aug, t_wload[p]])
            for c in range(NCH):
                hb_ = hbuf[hcn % 2]
                if hcn == 0:
                    h_load(0)
                    h_load(1)
                t_hl = t_hload[hcn]
                cols = slice(c * 512, (c + 1) * 512)
                for gi, wo in enumerate((0, 128, 384)):
                    acc = ip[:, ipn % 2, :]
                    tok = None
                    for kc in range(KC):
                        tok = K.op("pe", MM(acc, wbuf[p % 2][:, kc, wo:wo + 128], hb_[:, kc, :], kc == 0, kc == KC - 1),
                                   waits=[t_hl, t_wload[p]] + list(ev.get(ipn - 2, ())) if kc == 0 else (),
                                   tok=(kc == KC - 1))
                    if gi == 0:
                        a = K.op("act", ACT(QA[0:64, cols], acc[0:64, :], AF.Copy), waits=[tok] + evw, tok=True)
                        d = K.op("dve", CP(QB[64:128, cols], acc[64:128, :]), waits=[tok] + evw, tok=True)
                        ev[ipn] = (a, d)
                    elif gi == 1:
                        a = K.op("act", ACT(KA[0:64, cols], acc[0:64, :], AF.Copy), waits=[tok] + evw, tok=True)
                        d = K.op("dve", CP(KBt[64:128, cols], acc[64:128, :]), waits=[tok] + evw, tok=True)
                        ev[ipn] = (a, d)
                    else:
                        a = K.op("act", ACT(sg[:, cols], acc, AF.Silu), waits=[tok] + evw, tok=True)
                        ev[ipn] = (a,)
                    ipn += 1
                    K.cut(21 + gi, list(ev[ipn - 1]))
                acc = ip[:, ipn % 2, :].rearrange("p (b f) -> p b f", b=4)
                tok = None
                for bi in range(4):
                    for kc in range(KC):
                        last = (bi == 3 and kc == KC - 1)
                        tok = K.op("pe", MM(acc[:, bi, :], hb_[:, kc, bi * 128:(bi + 1) * 128],
                                            wbuf[p % 2][:, kc, 256:384], kc == 0, kc == KC - 1),
                                   waits=list(ev.get(ipn - 2, ())) if (bi == 0 and kc == 0) else (), tok=last)
                t_pechunk[hcn] = tok
                K.cut(24, [tok])
                a = K.op("act", ACT(VA[:, 4 * c:4 * c + 4, 0:64], acc[:, :, 0:64], AF.Copy), waits=[tok] + evw, tok=True)
                K.cut(25, [a])
                a = K.op("act", ACT(VB[:, 4 * c:4 * c + 4, 64:128], acc[:, :, 64:128], AF.Copy), tok=True)
                ev[ipn] = (a,)
                ipn += 1
                h_load(hcn + 2)
                hcn += 1
            t_wfree[p] = t_pechunk[hcn - 1]
            ev_done = list(ev[ipn - 1]) + list(ev[ipn - 2]) + list(ev[ipn - 3]) + list(ev[ipn - 4])

            K.cut(1, ev_done)
            tiles = []
            for X in (0, 1):
                for G in range(NCH):
                    for jj in range(2 * G + 2):
                        tiles.append((X, G, jj))

            def emit_qk(X, G, jj, n):
                Qt, Kt = (QA, KA) if X == 0 else (QB, KBt)
                rows = slice(0, 70) if X == 0 else slice(0, 128)
                sb_ = stv[n % NSB]
                tok = None
                first = True
                for sl in range(2):
                    j = 2 * jj + sl
                    r = j - 4 * G
                    w0 = [t_exp.get(n - NSB), t_aug] + ev_done if first else ()
                    first = False
                    kcols = slice(j * 128, (j + 1) * 128)
                    if r < 0:
                        tok = K.op("pe", MM(sb_[:, sl, :], Kt[rows, kcols], Qt[rows, G * 512:(G + 1) * 512], True, True),
                                   waits=w0, tok=(sl == 1))
                    else:
                        dsl = slice(r * 128, (r + 1) * 128)
                        K.op("pe", MM(sb_[:, sl, dsl], ident, cmask, True, False), waits=w0)
                        tok = K.op("pe", MM(sb_[:, sl, dsl], Kt[rows, kcols],
                                            Qt[rows, G * 512 + r * 128:G * 512 + (r + 1) * 128], False, True),
                                   tok=(sl == 1 and r == 3))
                        if r < 3:
                            tok = K.op("pe", MM(sb_[:, sl, (r + 1) * 128:512], Kt[rows, kcols],
                                                Qt[rows, G * 512 + (r + 1) * 128:(G + 1) * 512], True, True),
                                       tok=(sl == 1))
                return tok

            def emit_exp(X, G, jj, n, t_qk):
                buf = n % NSB
                sb_ = stv[buf]
                w = [t_qk, t_pv.get(n - NSB)]
                if 2 * jj + 1 < 4 * G:
                    return K.op("act", ACT(pt[buf][:], sb_, AF.Exp, scale=0.125), waits=w, tok=True)
                tok = None
                for sl in range(2):
                    r = 2 * jj + sl - 4 * G
                    tok = K.op("act", ACT(pt[buf][:, sl, r * 128:512], sb_[:, sl, r * 128:512], AF.Exp, scale=0.125),
                               waits=w if sl == 0 else (), tok=(sl == 1))
                return tok

            def emit_pv(X, G, jj, n, t_e, g):
                Vt = VA if X == 0 else VB
                buf = n % NSB
                tok = None
                for sl in range(2):
                    j = 2 * jj + sl
                    r = max(0, j - 4 * G)
                    w = [t_e, t_rel.get(g - 2)] if sl == 0 else ()
                    tok = K.op("pe", MM(ob[:, g % 2, r * 128:512], Vt[:, j, :], pt[buf][:, sl, r * 128:512],
                                        j == 0, j == 4 * G + 3), waits=w, tok=(sl == 1))
                return tok

            def emit_norm(X, G, g, t_last):
                nonlocal t_y
                o = ob[:, g % 2, :]
                cols = slice(G * 512, (G + 1) * 512)
                ro, rd = (slice(0, 64), slice(64, 128)) if X == 0 else (slice(64, 128), slice(0, 64))
                t = K.op("dve", RCP(rt[rd, :], o[rd, :]), waits=[t_last, t_y], tok=True)
                t = K.op("dve", TT(t1[ro, :], o[ro, :], rt[rd, :], ALU.mult), waits=[t], tok=True)
                t_rel[g] = t
                t_y = K.op("dve", TT(yp[p % 2][ro, cols], t1[ro, :], sg[ro, cols], ALU.mult),
                           waits=[t, t_ystore.get(p - 1)], tok=True)

            NTI = len(tiles)
            qk_tok = {}
            for i0 in range(min(NSB - 1, NTI)):
                qk_tok[i0] = emit_qk(*tiles[i0], tcnt + i0)
            for i in range(NTI):
                X, G, jj = tiles[i]
                n = tcnt + i
                if i + NSB - 1 < NTI:
                    qk_tok[i + NSB - 1] = emit_qk(*tiles[i + NSB - 1], n + NSB - 1)
                t_exp[n] = emit_exp(X, G, jj, n, qk_tok[i])
                g = gcnt + X * NCH + G
                t_pv[n] = emit_pv(X, G, jj, n, t_exp[n], g)
                if jj == 2 * G + 1:
                    emit_norm(X, G, g, t_pv[n])
            tcnt += NTI
            gcnt += 2 * NCH
            t_att_pe = t_pv[tcnt - 1]
            if fz is None:
                t_ystore[p] = K.dma("sp", y0T[p], yp[p % 2][:], s_y[p % 2], waits=[t_y])
            else:
                K.dma("sp", agin[p, 0, :, 128:SECL], yp[p % 2][:, 0:half], s_y[p % 2], waits=[t_y])
                t_ystore[p] = K.dma("sp", agin[p, 1, :, 0:SECL], yp[p % 2][:, half - 128:S], s_y[p % 2])
                K.raw("pool", (lambda pi: (lambda e: e.collective_compute(
                    "AllGather", ALU.bypass, replica_groups=fz["groups"],
                    ins=[agin[pi].rearrange("s f t -> (s f) t")],
                    outs=[agout[pi].rearrange("r s f t -> (r s f) t")])))(p),
                    fz["cc"], 1, waits=[t_ystore[p], t_zero])
        if fz is None:
            K._waits("sp", [t_ystore[NP - 1], t_ystore.get(NP - 2)])
            K.flush()
        else:
            fz["bar"] = [t_att_pe, t_exp[tcnt - 1], t_y, t_ystore[NP - 1], t_ystore.get(NP - 2)]


def make_cst():
    ident = np.eye(128, dtype=np.float32)
    s = np.arange(128)[:, None]
    t = np.arange(128)[None, :]
    cmask = np.where(s <= t, 0.0, NEG).astype(np.float32)
    return np.concatenate([ident, cmask], axis=1).astype(NPBF)


def kc_layout(w):
    n = w.shape[1]
    return np.ascontiguousarray(w.reshape(KC, 128, n).transpose(1, 0, 2))


def prep_A(x_b, norm_g0, w_in, b_f, hh, NP=8):
    W = 2048
    wl = []
    for p in range(NP):
        c0 = (hh * 16 + 2 * p) * 64
        cols = np.concatenate([w_in[:, c0:c0 + 128], w_in[:, W + c0:W + c0 + 128],
                               w_in[:, 2 * W + c0:2 * W + c0 + 128], w_in[:, 3 * W + c0:3 * W + c0 + 128]], axis=1)
        wl.append(kc_layout(cols))
    return {
        "x": np.ascontiguousarray(x_b),
        "g0": np.ascontiguousarray(norm_g0[None, :]),
        "wA": np.stack(wl),
        "wf": kc_layout(w_in[:, 4 * W + hh * 16:4 * W + hh * 16 + 16]),
        "bf": np.ascontiguousarray(b_f[hh * 16:hh * 16 + 16][:, None]),
        "cst": make_cst(),
    }


def build_B(NCHK=4, dbg=9):
    NTOK = 128 + 512 * NCHK
    nc = bass.Bass("TRN2", target_bir_lowering=False)
    yT_in = nc.dram_tensor("yT", [128, KC, NTOK], BF16, kind="ExternalInput").ap()
    xr = nc.dram_tensor("xr", [NTOK, D], F32, kind="ExternalInput").ap()
    wo0 = nc.dram_tensor("wo0", [4, 128, KC, 512], F32, kind="ExternalInput").ap()
    wi1 = nc.dram_tensor("wi1", [10, 128, KC, 512], F32, kind="ExternalInput").ap()
    wo1 = nc.dram_tensor("wo1", [4, 128, KC, 512], F32, kind="ExternalInput").ap()
    g1 = nc.dram_tensor("g1", [1, D], F32, kind="ExternalInput").ap()
    gf = nc.dram_tensor("gf", [1, D], F32, kind="ExternalInput").ap()
    snk = nc.dram_tensor("snk", [1, 32], F32, kind="ExternalInput").ap()
    cs = nc.dram_tensor("cs", [128, 2, NTOK], F32, kind="ExternalInput").ap()
    hbias = nc.dram_tensor("hbias", [128, 1], F32, kind="ExternalInput").ap()
    cst2 = nc.dram_tensor("cst2", [128, 1280], BF16, kind="ExternalInput").ap()
    out = nc.dram_tensor("out", [512 * NCHK, D], F32, kind="ExternalOutput").ap()
    wsc = nc.dram_tensor("wsc", [18, 128, KC, 512], BF16).ap()
    K = KB(nc)
    K.dbg = dbg
    try:
        with K.es:
            _build_B_body(K, nc, NCHK, NTOK, yT_in, xr, wo0, wi1, wo1, g1, gf, snk, cs, hbias, cst2, out, wsc)
    except _Cut:
        pass
    return nc


def _build_B_body(K, nc, NCHK, NTOK, yT_in, xr, wo0, wi1, wo1, g1, gf, snk, cs, hbias, cst2, out, wsc, fz=None):
    cst_sb = K.sb("cst_sb", [128, 1280], BF16)
    ident = cst_sb[:, 0:128]
    mrep = [cst_sb[:, 640:1152], cst_sb[:, 128:640]]
    Pm = cst_sb[:, 1152:1280]
    gbc1 = K.sb("gbc1", [128, D], F32)
    gbcf = K.sb("gbcf", [128, D], F32)
    es_bc = K.sb("es_bc", [128, 32], F32)
    hb_sb = K.sb("hb_sb", [128, 1], F32)
    wbuf = [K.sb("wbuf%d" % i, [128, KC, 512], BF16) for i in range(2)]
    yq = K.sb("yq", [128, KC, 512], BF16)
    hy = K.sb("hy", [128, KC, 512], BF16)
    sg1 = K.sb("sg1", [128, KC, 512], BF16)
    xs = K.sb("xs", [128, 4, D], F32)
    kTe = K.sb("kTe", [128, 4, 640], BF16)
    kTo = K.sb("kTo", [128, 4, 640], BF16)
    Vt = K.sb("Vt", [128, 5, 4, 128], BF16)
    cs_sb = K.sb("cs_sb", [128, 2, 512], F32)
    qb = [K.sb("qb%d" % i, [128, 512], BF16) for i in range(2)]
    t1r = [K.sb("t1r%d" % i, [128, 512], F32) for i in range(2)]
    t2r = [K.sb("t2r%d" % i, [128, 512], F32) for i in range(2)]
    pt1 = [K.sb("pt1_%d" % i, [128, 2, 4, 128], BF16) for i in range(2)]
    rt2 = [K.sb("rt%d" % i, [128, 4, 128], F32) for i in range(2)]
    t1 = K.sb("t1", [128, 4, 128], F32)
    dsb = [K.sb("dsb%d" % i, [128, 512], F32) for i in range(2)] if F_BATT else None
    pa = K.ps("pa", [128, 4, 512], F32)
    ob1 = K.ps("ob1", [128, 2, 512], F32)
    tpp = K.ps("tpp", [128, 2, 512], F32)
    tp1 = [tpp[:].rearrange("p a b -> p (a b)").bitcast(BF16).rearrange("p (k t) -> p k t", k=KC)]
    st1 = [pa[:, 2 * i:2 * i + 2, :].rearrange("p k (g t) -> p k g t", g=4) for i in range(2)]
    ob1v = [ob1[:, 0, :], ob1[:, 1, :], tpp[:, 0, :], tpp[:, 1, :]]
    NOB = 2
    s_c = K.sem("s_c")
    s_wc = [K.sem("s_wc%d" % i) for i in range(2)]
    s_ws = [K.sem("s_ws%d" % i) for i in range(2)]
    s_wl = [K.sem("s_wl%d" % i) for i in range(2)]
    s_in = K.sem("s_in")
    s_in2 = K.sem("s_in2")
    s_in3 = [K.sem("s_in3_%d" % i) for i in range(4)]
    s_out = [K.sem("s_out%d" % i) for i in range(4)]

    K.dma("sp", cst_sb[:], cst2, s_c)
    K.dma("sp", gbc1[:], g1.partition_broadcast(128), s_c)
    K.dma("sp", gbcf[:], gf.partition_broadcast(128), s_c)
    K.dma("sp", es_bc[:], snk.partition_broadcast(128), s_c)
    if fz is not None:
        sel = K.sb("sel", [128, 2], F32)
        K.dma("sp", sel[:], fz["sel"], s_c)
    t_c = K.dma("sp", hb_sb[:], hbias, s_c)
    t_es = K.op("act", ACT(es_bc[:], es_bc[:], AF.Exp), waits=[t_c], tok=True)
    K.op("pool", MSET(kTe[:], 0.0))
    K.op("pool", MSET(kTo[:], 0.0))
    t_set = K.op("pool", MSET(Vt[:, :, :, 64:128], 1.0), tok=True)

    t_wst = {}
    if fz is None:
        for t in range(18):
            src = wo0[t] if t < 4 else (wi1[t - 4] if t < 14 else wo1[t - 14])
            tl = None
            for k in range(8):
                tl = K.dma("pool", wbuf[t % 2][:, 2 * k:2 * k + 2, :], src[:, 2 * k:2 * k + 2, :], s_wc[t % 2],
                           waits=[t_wst.get(t - 2)])
            t_wst[t] = K.dma("sp", wsc[t], wbuf[t % 2][:], s_ws[t % 2], waits=[tl])
        t_P = [t_wst[16], t_wst[17]]
    else:
        t_P = [fz["t_P"]]
    K.cut(0, t_P)

    NTn = NormT(K, gbc1, ident, tp1, "n1", nhb=1)
    NTn.junk = NTn.junk

    st = {"ipn": 0, "rn": 0, "ev": {}, "t_pool_rope": {}, "t_pm": {}, "t_d2": {}, "pending": None,
          "tile": 0, "t_exp": {}, "t_pv": {}, "t_norm": {}, "t_lastattn": None, "t_y": None,
          "t_xfree": None, "t_yqfree": None, "t_hyfree": None, "t_out": {}}
    steps = []

    st["ypre"] = {}
    st["t_dc"] = {}
    st["t_ln"] = {}

    def y_prefetch(g):
        nt_ = 128 if g == 0 else 512
        t0_ = 0 if g == 0 else (1 + 4 * (g - 1)) * 128
        agout = fz["agout"]
        tl_ = None
        for sec, dstt in ((0, yq), (1, sg1)):
            for r in range(2):
                tl_ = K.dma("sp", dstt[:, r * 8:(r + 1) * 8, 0:nt_],
                            agout[:, r, sec, :, t0_:t0_ + nt_].rearrange("p f t -> f p t"), s_in,
                            waits=[st["t_yqfree"], st["t_yqfree_pe"], st["t_y"], fz["t_cc"]])
        st["ypre"][g] = {"t_dma": tl_, "nt": nt_}

    def y_blend(g):
        e_ = st["ypre"][g]
        nt_ = e_["nt"]
        b1 = K.op("dve", TS(yq[:, :, 0:nt_], yq[:, :, 0:nt_], sel[:, 0:1], ALU.mult), waits=[e_["t_dma"], t_c], tok=True)
        e_["t_y"] = K.op("dve", STT(yq[:, :, 0:nt_], sg1[:, :, 0:nt_], sel[:, 1:2], yq[:, :, 0:nt_],
                                    ALU.mult, ALU.add), waits=[b1], tok=True)

    def group_steps(gi):
        halo = (gi == 0)
        nb = 1 if halo else 4
        NT = nb * 128
        lb0 = 0 if halo else 1 + 4 * (gi - 1)
        tk0 = lb0 * 128
        G = {}

        def f_out0(n):
            def fn(wb, tl):
                if n == 0 and fz is not None:
                    if gi not in st["ypre"]:
                        y_prefetch(gi)
                    if "t_y" not in st["ypre"][gi]:
                        y_blend(gi)
                    G["t_y"] = st["ypre"][gi]["t_y"]
                if n == 0:
                    if fz is None:
                        G["t_y"] = K.dma("sp", yq[:, :, 0:NT], yT_in[:, :, tk0:tk0 + NT], s_in,
                                         waits=[st["t_yqfree"], st["t_yqfree_pe"]])
                    G["t_cs"] = K.dma("sp", cs_sb[:, :, 0:NT], cs[:, :, tk0:tk0 + NT], s_in2, waits=[st.get("t_lastrope")])
                    G["t_x"] = {}
                    for bx in range(nb):
                        G["t_x"][bx] = K.dma("sp", xs[:, bx, :], xr[tk0 + bx * 128:tk0 + (bx + 1) * 128, :],
                                             s_in3[bx], waits=[st["t_out"].get(bx), st["t_xfree"] if bx == 0 else None])
                    G["t_add"] = {}
                tok = None
                for blk in range(nb):
                    acc = pa[:, st["ipn"] % 2, :]
                    for kc in range(KC):
                        tok = K.op("pe", MM(acc, yq[:, kc, blk * 128:(blk + 1) * 128], wb[:, kc, :], kc == 0, kc == KC - 1),
                                   waits=[tl, G["t_y"], st["t_lastattn"]] + list(st["ev"].get(st["ipn"] - 2, ())) if kc == 0 else (),
                                   tok=(kc == KC - 1))
                    xv = xs[:, blk, n * 512:(n + 1) * 512]
                    d = K.op("dve", TT(xv, acc, xv, ALU.add), waits=[tok, G["t_x"][blk]], tok=True)
                    st["ev"][st["ipn"]] = (d,)
                    G["t_add"][blk] = d
                    st["ipn"] += 1
                if n == 3:
                    st["t_yqfree_pe"] = tok
                    pre_ = NTn.stats_dve(NTn.stats_act(xs[:, 0, :], [G["t_add"][0], t_c]))
                    for blk in range(nb):
                        sa_ = NTn.stats_act(xs[:, blk + 1, :], [G["t_add"][blk + 1], t_c]) if (blk + 1 < nb and F_BNORM) else None
                        G["t_h"] = NTn.emit(xs[:, blk, :], hy[:, :, blk * 128:(blk + 1) * 128],
                                            waits=[], dst_waits=[st["t_hyfree_pe"]], pre=pre_)
                        if sa_ is not None:
                            pre_ = NTn.stats_dve(sa_)
                        elif blk + 1 < nb:
                            pre_ = NTn.stats_dve(NTn.stats_act(xs[:, blk + 1, :], [G["t_add"][blk + 1], t_c]))
                    st["t_xfree"] = G["t_h"]
                return tok
            return fn

        def rope_first(acc, tok, dst, dst_waits):
            rn = st["rn"]
            a = K.op("act", ACT(qb[rn % 2][:, 0:NT], acc[:, 0:NT], AF.Copy), waits=[tok, st["t_pm"].get(rn - 2)], tok=True)
            d1 = K.op("dve", TT(t1r[rn % 2][:, 0:NT], acc[:, 0:NT], cs_sb[:, 0, 0:NT], ALU.mult),
                      waits=[tok, a, st["t_pool_rope"].get(rn - 2), G["t_cs"]], tok=True)
            st["pending"] = (rn, a, d1, dst, dst_waits)
            st["rn"] += 1
            return (a, d1)

        def rope_second():
            if st["pending"] is None:
                return
            rn, a, d1, dst, dst_waits = st["pending"]
            st["pending"] = None
            pp = pa[:, 2 + rn % 2, :]
            pm = K.op("pe", MM(pp[:, 0:NT], Pm, qb[rn % 2][:, 0:NT], True, True), waits=[a, st["t_d2"].get(rn - 2)], tok=True)
            st["t_pm"][rn] = pm
            d2 = K.op("dve", TT(t2r[rn % 2][:, 0:NT], pp[:, 0:NT], cs_sb[:, 1, 0:NT], ALU.mult),
                      waits=[pm, st["t_pool_rope"].get(rn - 2)], tok=True)
            st["t_d2"][rn] = d2
            pl = None
            for (r0_, r1_, dap) in dst:
                pl = K.op("dve", TT(dap, t1r[rn % 2][r0_:r1_, 0:NT], t2r[rn % 2][r0_:r1_, 0:NT], ALU.add),
                          waits=[d1, d2] + list(dst_waits), tok=True)
            st["t_pool_rope"][rn] = pl
            st["t_lastrope"] = pl

        def f_in(wt):
            def fn(wb, tl):
                tok = None
                if halo and wt == 4 and fz is not None and NCHK >= 1:
                    y_prefetch(1)
                if wt == 9:
                    for blk in range(nb):
                        acc = pa[:, st["ipn"] % 2, 0:256]
                        for kc in range(KC):
                            tok = K.op("pe", MM(acc, hy[:, kc, blk * 128:(blk + 1) * 128], wb[:, kc, 0:256], kc == 0, kc == KC - 1),
                                       waits=[tl, G["t_h"]] + list(st["ev"].get(st["ipn"] - 2, ())) if kc == 0 else (),
                                       tok=(kc == KC - 1))
                        slot = 0 if halo else 1 + blk
                        a = K.op("act", ACT(Vt[:, slot, :, 0:64], acc.rearrange("p (h d) -> p h d", h=4), AF.Copy),
                                 waits=[tok, t_set, st["t_lastattn"], st.get("t_ring")], tok=True)
                        st["ev"][st["ipn"]] = (a,)
                        st["ipn"] += 1
                    G["t_v"] = a
                    rope_second()
                    if halo and fz is not None and 1 in st["ypre"]:
                        y_blend(1)
                    return tok
                for fl in range(4):
                    acc = pa[:, st["ipn"] % 2, :]
                    for kc in range(KC):
                        tok = K.op("pe", MM(acc[:, 0:NT], wb[:, kc, fl * 128:(fl + 1) * 128], hy[:, kc, 0:NT], kc == 0, kc == KC - 1),
                                   waits=[tl, G["t_h"]] + list(st["ev"].get(st["ipn"] - 2, ())) if kc == 0 else (),
                                   tok=(kc == KC - 1))
                    rope_second()
                    if wt < 4:
                        f = 4 * wt + fl
                        evt = rope_first(acc, tok, [(0, 128, yq[:, f, 0:NT])], [st["t_yqfree_pe"]])
                    elif wt == 4:
                        k0 = 0 if halo else 128
                        evt = rope_first(acc, tok, [(0, 64, kTe[0:64, fl, k0:k0 + NT]), (64, 128, kTo[64:128, fl, k0:k0 + NT])],
                                         [st["t_lastattn"], st.get("t_ring"), t_set])
                    else:
                        f = 4 * (wt - 5) + fl
                        a = K.op("act", ACT(sg1[:, f, 0:NT], acc[:, 0:NT], AF.Silu), waits=[tok, st.get("t_y1"), G["t_y"]], tok=True)
                        evt = (a,)
                    st["ev"][st["ipn"]] = evt
                    st["ipn"] += 1
                return tok
            return fn

        def attention():
            rope_second()
            q_ready = [st["t_lastrope"], G["t_v"]]
            tiles = [(blk, h, hf) for blk in range(4) for h in range(4) for hf in range(2)]

            def e_qk(blk, h, hf, n):
                buf = n % 2
                tok = None
                for kbi in range(2):
                    kcol = blk * 128 + kbi * 128
                    K.op("pe", MM(st1[buf][:, kbi].rearrange("p g t -> p (g t)"), ident, mrep[kbi], True, False),
                         waits=[st["t_exp"].get(n - 2)] + q_ready + list(st["ev"].get(st["ipn"] - 1, ())) + list(st["ev"].get(st["ipn"] - 2, ()))
                         if kbi == 0 else ())
                    for gg in range(4):
                        hq = 8 * h + 4 * hf + gg
                        f = hq // 2
                        kz = kTe if hq % 2 == 0 else kTo
                        tok = K.op("pe", MM(st1[buf][:, kbi, gg, :], kz[:, h, kcol:kcol + 128],
                                            yq[:, f, blk * 128:(blk + 1) * 128], False, True),
                                   tok=(kbi == 1 and gg == 3))
                return tok

            def e_exp(blk, h, hf, n, tq):
                buf = n % 2
                w = [tq, st["t_pv"].get(n - 2)]
                if gi == 1 and blk == 0:
                    K.op("act", ACT(pt1[buf][:, 0], st1[buf][:, 0], AF.Exp, scale=0.125, bias=hb_sb[:, 0:1]), waits=w + [t_c])
                    return K.op("act", ACT(pt1[buf][:, 1], st1[buf][:, 1], AF.Exp, scale=0.125), tok=True)
                return K.op("act", ACT(pt1[buf][:], st1[buf], AF.Exp, scale=0.125), waits=w, tok=True)

            def e_pv(blk, h, hf, n, te):
                buf = n % 2
                tok = None
                for gg in range(4):
                    for kbi in range(2):
                        tok = K.op("pe", MM(ob1v[n % NOB][:, gg * 128:(gg + 1) * 128], Vt[:, blk + kbi, h, :], pt1[buf][:, kbi, gg, :],
                                            kbi == 0, kbi == 1),
                                   waits=[te, st["t_norm"].get(n - NOB)] if (gg == 0 and kbi == 0) else (),
                                   tok=(gg == 3 and kbi == 1))
                return tok

            def e_dcopy(n, tp_):
                st["t_dc"][n] = K.op("dve", CP(dsb[n % 2][64:128, :], ob1v[n % NOB][64:128, :]),
                                     waits=[tp_, st["t_ln"].get(n - 2)], tok=True)

            def e_norm(blk, h, hf, n, tp_):
                o = ob1v[n % NOB]
                rt = rt2[n % 2]
                cols = slice(blk * 128, (blk + 1) * 128)
                a = None
                dsrc = dsb[n % 2] if F_BATT else o
                if F_BATT:
                    tp_ = st["t_dc"][n]
                for gg in range(4):
                    hq = 8 * h + 4 * hf + gg
                    a = K.op("act", ACT(rt[64:128, gg, :], dsrc[64:128, gg * 128:(gg + 1) * 128], AF.Ln,
                                        bias=es_bc[64:128, hq:hq + 1], scale=1.0),
                             waits=[tp_, t_es, st["t_norm"].get(n - 2)] if gg == 0 else (), tok=(gg == 3))
                st["t_ln"][n] = a
                a = K.op("act", ACT(rt[64:128].rearrange("p g t -> p (g t)"), rt[64:128].rearrange("p g t -> p (g t)"),
                                    AF.Exp, scale=-1.0), waits=[a], tok=True)
                K.op("dve", TT(t1[0:64].rearrange("p g t -> p (g t)"), o[0:64, :],
                               rt[64:128].rearrange("p g t -> p (g t)"), ALU.mult), waits=[a, st["t_y"]])
                d = K.op("dve", TT(t1[64:128].rearrange("p g t -> p (g t)"), o[0:64, :],
                                   rt[64:128].rearrange("p g t -> p (g t)"), ALU.mult), tok=True)
                st["t_norm"][n] = d
                K.cut(203, [d])
                f0 = (8 * h + 4 * hf) // 2
                d = K.op("dve", TT(hy[0:64, f0:f0 + 2, cols], t1[0:64, 0:4:2, :], sg1[0:64, f0:f0 + 2, cols], ALU.mult),
                         waits=[d], tok=True)
                d = K.op("dve", TT(hy[64:128, f0:f0 + 2, cols], t1[64:128, 1:4:2, :], sg1[64:128, f0:f0 + 2, cols], ALU.mult),
                         tok=True)
                st["t_y"] = d

            n0 = st["tile"]
            tq = {0: e_qk(*tiles[0], n0)}
            for i, (blk, h, hf) in enumerate(tiles):
                n = n0 + i
                if i + 1 < len(tiles):
                    tq[i + 1] = e_qk(*tiles[i + 1], n + 1)
                st["t_exp"][n] = e_exp(blk, h, hf, n, tq[i])
                K.cut(200, [st["t_exp"][n]])
                st["t_pv"][n] = e_pv(blk, h, hf, n, st["t_exp"][n])
                K.cut(201, [st["t_pv"][n]])
                if not F_BATT:
                    e_norm(blk, h, hf, n, st["t_pv"][n])
                else:
                    e_dcopy(n, st["t_pv"][n])
                    if i > 0:
                        e_norm(*tiles[i - 1], n - 1, st["t_pv"][n - 1])
                    if i == len(tiles) - 1:
                        e_norm(blk, h, hf, n, st["t_pv"][n])
                K.cut(204, [st["t_y"]])
            st["tile"] += len(tiles)
            K.cut(205, [st["t_y"]])
            st["t_lastattn"] = st["t_pv"][st["tile"] - 1]
            st["t_yqfree"] = st["t_lastattn"]
            st["t_y1"] = st["t_y"]
            K.op("pool", CP(kTe[:, :, 0:128], kTe[:, :, 512:640]), waits=[st["t_lastattn"]])
            K.op("pool", CP(kTo[:, :, 0:128], kTo[:, :, 512:640]))
            r2_ = K.op("pool", CP(Vt[:, 0, :, 0:64], Vt[:, 4, :, 0:64]), tok=True)
            st["t_ring"] = r2_
            K.cut(206, [r2_])

        def f_out1(n):
            def fn(wb, tl):
                if n == 0:
                    attention()
                    G["t_add2"] = {}
                    if fz is not None and gi + 1 <= NCHK:
                        y_prefetch(gi + 1)
                tok = None
                for blk in range(4):
                    acc = pa[:, st["ipn"] % 2, :]
                    for kc in range(KC):
                        tok = K.op("pe", MM(acc, hy[:, kc, blk * 128:(blk + 1) * 128], wb[:, kc, :], kc == 0, kc == KC - 1),
                                   waits=[tl, st["t_y"], st["t_exp"][st["tile"] - 1], st["t_exp"][st["tile"] - 2]]
                                   + list(st["ev"].get(st["ipn"] - 2, ())) if kc == 0 else (),
                                   tok=(kc == KC - 1))
                    xv = xs[:, blk, n * 512:(n + 1) * 512]
                    d = K.op("dve", TT(xv, acc, xv, ALU.add), waits=[tok], tok=True)
                    st["ev"][st["ipn"]] = (d,)
                    G["t_add2"][blk] = d
                    st["ipn"] += 1
                if n == 2 and fz is not None and (gi + 1) in st["ypre"]:
                    y_blend(gi + 1)
                if n == 3:
                    st["t_hyfree_pe"] = tok
                    t_o = None
                    sas = [NTn.stats_act(xs[:, blk, :], [G["t_add2"][blk]]) for blk in range(4)]
                    for blk in range(4):
                        t, col = NTn.stats_dve(sas[blk])
                        d = K.op("dve", STT(xs[:, blk, :], xs[:, blk, :], NTn.rstd[:, col:col + 1], gbcf[:], ALU.mult, ALU.mult),
                                 waits=[t], tok=True)
                        r0 = (lb0 - 1 + blk) * 128
                        st["t_out"][blk] = K.dma("sp", out[r0:r0 + 128, :], xs[:, blk, :], s_out[blk], waits=[d])
                return tok
            return fn

        for n in range(4):
            steps.append((n, f_out0(n)))
        for wt in ((4, 9) if halo else range(10)):
            steps.append((4 + wt, f_in(wt)))
        if not halo:
            for n in range(4):
                steps.append((14 + n, f_out1(n)))

    st["t_hyfree_pe"] = None
    st["t_yqfree_pe"] = None
    for gi in range(1 + NCHK):
        group_steps(gi)

    t_load = {}
    t_use = {}

    def issue(i):
        t_load[i] = K.dma("sp", wbuf[i % 2][:], wsc[steps[i][0]], s_wl[i % 2], waits=t_P + [t_use.get(i - 2)])

    issue(0)
    for i, (widx, fn) in enumerate(steps):
        if i + 1 < len(steps):
            issue(i + 1)
        t_use[i] = fn(wbuf[i % 2], t_load[i])
        K.cut(100 + i, [t_use[i]])
    K._waits("sp", [st["t_out"].get(b_) for b_ in range(4)])
    K.flush()


def build_fused(S=4096, NP=8, groups=None):
    NB = S // 128
    NCH = S // 512
    half = S // 2
    NCHK = half // 512
    NTOK = half + 128
    SECL = NTOK
    if groups is None:
        groups = [[0, 1], [2, 3], [4, 5], [6, 7]]
    nc = bass.Bass("TRN2", target_bir_lowering=False)
    di = lambda name, shape, dt: nc.dram_tensor(name, shape, dt, kind="ExternalInput").ap()
    x = di("x", [S, D], F32)
    g0 = di("g0", [1, D], F32)
    wA = di("wA", [NP, 128, KC, 512], F32)
    wf = di("wf", [128, KC, 16], F32)
    bfv = di("bf", [16, 1], F32)
    cst = di("cst", [128, 256], BF16)
    xr = di("xr", [NTOK, D], F32)
    wo0 = di("wo0", [4, 128, KC, 512], F32)
    wi1 = di("wi1", [10, 128, KC, 512], F32)
    wo1 = di("wo1", [4, 128, KC, 512], F32)
    g1 = di("g1", [1, D], F32)
    gf = di("gf", [1, D], F32)
    snk = di("snk", [1, 32], F32)
    cs = di("cs", [128, 2, NTOK], F32)
    hbias = di("hbias", [128, 1], F32)
    cst2 = di("cst2", [128, 1280], BF16)
    sel_d = di("sel", [128, 2], F32)
    out = nc.dram_tensor("out", [half, D], F32, kind="ExternalOutput").ap()
    hT_d = nc.dram_tensor("hT_d", [NCH, 128, KC, 512], BF16).ap()
    caug = nc.dram_tensor("caug", [16, 2, 3, S], BF16).ap()
    wsc = nc.dram_tensor("wsc", [18, 128, KC, 512], BF16).ap()
    agin = nc.dram_tensor("agin", [NP, 2, 128, SECL], BF16).ap()
    agout = nc.dram_tensor("agout", [NP, 2, 2, 128, SECL], BF16).ap()

    K = KB(nc)
    with K.es:
        s_wp = K.sem("s_wp")
        cc = K.sem("cc")
        fz = {"agin": agin, "agout": agout, "groups": groups, "cc": cc, "sel": sel_d}
        per = [(18 * p) // NP for p in range(NP + 1)]

        def emit_P(p):
            for t in range(per[p], per[p + 1]):
                src = wo0[t] if t < 4 else (wi1[t - 4] if t < 14 else wo1[t - 14])
                for k in range(8):
                    fz["t_P"] = K.dma("pool", wsc[t][:, 2 * k:2 * k + 2, :], src[:, 2 * k:2 * k + 2, :], s_wp)

        fz["emit_P"] = emit_P
        with ExitStack() as esA:
            K.cur = esA
            K.prefix = "A_"
            _build_A_body(K, nc, S, NP, NB, NCH, x, g0, wA, wf, bfv, cst, None, hT_d, caug, 9, fz=fz)
            K.flush()
        fz["t_cc"] = (cc, cc.n)
        K.barrier(fz["bar"])
        with ExitStack() as esB:
            K.cur = esB
            K.prefix = "B_"
            _build_B_body(K, nc, NCHK, NTOK, None, xr, wo0, wi1, wo1, g1, gf, snk, cs, hbias, cst2, out, wsc, fz=fz)
        K.cur = None
    return nc


def make_cst2():
    ident = np.eye(128, dtype=np.float32)
    s = np.arange(128)[:, None]
    t = np.arange(128)[None, :]
    mcur = np.where(s <= t, 0.0, NEG).astype(np.float32)
    mprev = np.where(s > t, 0.0, NEG).astype(np.float32)
    Pm = np.zeros((128, 128), np.float32)
    for do in range(128):
        d = do % 64
        if d < 16:
            di = do + 8 if d < 8 else do - 8
            Pm[di, do] = 1.0
    return np.concatenate([ident, np.tile(mcur, (1, 4)), np.tile(mprev, (1, 4)), Pm], axis=1).astype(NPBF)


def rope_tables(pos):
    half = 8
    inv_freq = (np.float32(500000.0) ** (-np.arange(half, dtype=np.float32) / np.float32(half))).astype(np.float32)
    ang = (pos.astype(np.float32)[:, None] * inv_freq[None, :]).astype(np.float32)
    cos = np.cos(ang).astype(np.float32).T
    sin = np.sin(ang).astype(np.float32).T
    n = pos.shape[0]
    C = np.ones((128, n), np.float32)
    Sg = np.zeros((128, n), np.float32)
    for base in (0, 64):
        C[base:base + 8] = cos
        C[base + 8:base + 16] = cos
        Sg[base:base + 8] = -sin
        Sg[base + 8:base + 16] = sin
    return np.ascontiguousarray(np.stack([C, Sg], axis=1))


def tile_layout(w):
    n = w.shape[1] // 512
    return np.ascontiguousarray(w.reshape(KC, 128, n, 512).transpose(2, 1, 0, 3))


def prep_B_weights(w_out0, w_in1, w_out1):
    WQ, WK = 2048, 256
    q = w_in1[:, :WQ]
    k = w_in1[:, WQ:WQ + WK]
    v = w_in1[:, WQ + WK:WQ + 2 * WK]
    g = w_in1[:, WQ + 2 * WK:]
    kdup = np.concatenate([np.concatenate([k[:, h * 64:(h + 1) * 64]] * 2, axis=1) for h in range(4)], axis=1)
    vpad = np.concatenate([v, np.zeros((2048, 256), np.float32)], axis=1)
    wi = np.concatenate([q, kdup, g, vpad], axis=1)
    return {"wo0": tile_layout(w_out0), "wi1": tile_layout(wi), "wo1": tile_layout(w_out1)}


def prep_B(yT_full, x_rows, pos, hh, wts, g1, gf, sinks):
    ntok = x_rows.shape[0]
    d = dict(wts)
    d["yT"] = np.ascontiguousarray(yT_full.reshape(KC, 128, ntok).transpose(1, 0, 2))
    d["xr"] = np.ascontiguousarray(x_rows)
    d["g1"] = np.ascontiguousarray(g1[None, :])
    d["gf"] = np.ascontiguousarray(gf[None, :])
    d["snk"] = np.ascontiguousarray(sinks[None, :])
    d["cs"] = rope_tables(pos)
    d["hbias"] = np.full((128, 1), NEG if hh == 0 else 0.0, np.float32)
    d["cst2"] = make_cst2()
    return d


def kernel(x, norm_g, fox_w_in, fox_b_f, fox_w_out, swa_w_in, swa_sinks, swa_w_out, final_g):
    x = np.asarray(x, dtype=np.float32)
    norm_g = np.asarray(norm_g, dtype=np.float32)
    fox_w_in = np.asarray(fox_w_in, dtype=np.float32)
    fox_b_f = np.asarray(fox_b_f, dtype=np.float32)
    fox_w_out = np.asarray(fox_w_out, dtype=np.float32)
    swa_w_in = np.asarray(swa_w_in, dtype=np.float32)
    swa_sinks = np.asarray(swa_sinks, dtype=np.float32)
    swa_w_out = np.asarray(swa_w_out, dtype=np.float32)
    final_g = np.asarray(final_g, dtype=np.float32)
    B, S, _ = x.shape
    ncores = 8
    half = S // 2
    nc = build_fused(S, 8)
    wA_parts = [prep_A(x[0], norm_g[0], fox_w_in[0], fox_b_f[0], hh) for hh in range(2)]
    for wp in wA_parts:
        wp.pop("x")
    wtsB = prep_B_weights(fox_w_out[0], swa_w_in[0], swa_w_out[0])
    maps = []
    for c in range(ncores):
        b, hh = c // 2, c % 2
        maps.append(prep_fused(x[b], hh, S, wA_parts, wtsB, norm_g, final_g, swa_sinks[0]))
    res = run_bass_kernel_spmd(nc, maps, core_ids=list(range(ncores)))
    out = np.zeros((B, S, D), np.float32)
    for c in range(ncores):
        b, hh = c // 2, c % 2
        out[b, hh * half:(hh + 1) * half] = np.asarray(res.results[c]["out"]).reshape(half, D)
    return out


def prep_fused(x_b, hh, S, wA_parts, wtsB, norm_g, final_g, sinks):
    half = S // 2
    ntok = half + 128
    t0 = hh * half - 128
    toks = np.arange(t0, t0 + ntok)
    valid = toks >= 0
    xr = np.zeros((ntok, D), np.float32)
    xr[valid] = x_b[toks[valid]]
    m = dict(wA_parts[hh])
    m["x"] = np.ascontiguousarray(x_b)
    m.update(wtsB)
    m["xr"] = xr
    m["g1"] = np.ascontiguousarray(norm_g[1][None, :])
    m["gf"] = np.ascontiguousarray(final_g[None, :])
    m["snk"] = np.ascontiguousarray(sinks[None, :])
    m["cs"] = rope_tables(toks.astype(np.float32))
    m["hbias"] = np.full((128, 1), NEG if hh == 0 else 0.0, np.float32)
    m["cst2"] = make_cst2()
    sel = np.zeros((128, 2), np.float32)
    sel[:, hh] = 1.0
    m["sel"] = sel
    return m
```
